# Optimizing a Trainium2 kernel written in Bass

```python
import math
import jax, jax.numpy as jnp
from jax import lax
import numpy as np

D_MODEL = 2048
BATCH = 4
SEQ = 2048
DEPTH = 2
DEC_BATCH = 8
DEC_SEQ = 32
PAST_LEN = 1024

CHUNK = 64
Q_BLOCK = 128
N_MIXERS = 2
N_MLA = (DEPTH + 1) // 2
N_DIFF = DEPTH // 2
ALPHA = (2 * DEPTH) ** 0.25
BETA = (8 * DEPTH) ** -0.25
LN_EPS = 1e-5
RMS_EPS = 1e-6
NEG = -1e30
MLA_HEADS = 16
MLA_Q_LORA = 512
MLA_KV_LORA = 512
MLA_NOPE = 128
MLA_ROPE = 64
MLA_V = 128
ROPE_THETA = 10000.0
DIFF_HEADS = 8
DIFF_QK = 128
DIFF_V = 256
PEER_HEADS = 8
PEER_TOPK = 16
N_KEYS = 128
N_EXPERTS = N_KEYS * N_KEYS
PEER_DKEY = 256
PEER_BLOCK = 128

kernel_name = "hybrid_mla_diffattn_peer_stream_step"


def layer_norm(x, g, b):
    xf = x.astype(jnp.float32)
    mu = xf.mean(-1, keepdims=True)
    var = jnp.square(xf - mu).mean(-1, keepdims=True)
    return ((xf - mu) * lax.rsqrt(var + LN_EPS)).astype(x.dtype) * g + b


def rms_norm(x, g):
    xf = x.astype(jnp.float32)
    return (xf * lax.rsqrt(jnp.square(xf).mean(-1, keepdims=True) + RMS_EPS)).astype(x.dtype) * g


def rope(x, pos):
    half = MLA_ROPE // 2
    inv = ROPE_THETA ** (-jnp.arange(half, dtype=jnp.float32) / half)
    ang = pos.astype(jnp.float32)[:, None] * inv
    ang = ang.reshape(ang.shape[0], *([1] * (x.ndim - 3)), half)
    cos, sin = jnp.cos(ang).astype(x.dtype), jnp.sin(ang).astype(x.dtype)
    x1, x2 = x[..., :half], x[..., half:]
    return jnp.concatenate([x1 * cos - x2 * sin, x2 * cos + x1 * sin], -1)


def chunk_mask(q_pos, k_pos):
    return (k_pos[None, :] // CHUNK) <= (q_pos[:, None] // CHUNK)


def alibi_slopes(n):
    return 2.0 ** (-8.0 * jnp.arange(1, n + 1, dtype=jnp.float32) / n)


def map_query_blocks(fn, qs, q_pos):
    tq = q_pos.shape[0]
    qb = min(Q_BLOCK, tq)
    nb = tq // qb
    qs_b = tuple(jnp.moveaxis(q.reshape(q.shape[0], nb, qb, *q.shape[2:]), 1, 0) for q in qs)
    pos_b = q_pos.reshape(nb, qb)
    out = lax.map(lambda a: fn(a[0], a[1]), (qs_b, pos_b))
    out = jnp.moveaxis(out, 0, 1)
    return out.reshape(out.shape[0], tq, *out.shape[3:])


def mla_project(x, pos, w_dqkv, g_q, w_uq, g_kv):
    b, t, _ = x.shape
    lat = x @ w_dqkv
    cq, ckv, kr = jnp.split(lat, [MLA_Q_LORA, MLA_Q_LORA + MLA_KV_LORA], -1)
    q = (rms_norm(cq, g_q) @ w_uq).reshape(b, t, MLA_HEADS, MLA_NOPE + MLA_ROPE)
    q_nope, q_rope = q[..., :MLA_NOPE], rope(q[..., MLA_NOPE:], pos)
    return q_nope, q_rope, rms_norm(ckv, g_kv), rope(kr, pos)


def mla_attend(q_nope, q_rope, q_pos, ckv, k_rope, k_pos, w_ukv, w_o):
    b, tk, _ = ckv.shape
    kv = (ckv @ w_ukv).reshape(b, tk, MLA_HEADS, MLA_NOPE + MLA_V)
    k_nope, v = kv[..., :MLA_NOPE], kv[..., MLA_NOPE:]
    scale = (MLA_NOPE + MLA_ROPE) ** -0.5

    def block(qs, qp):
        qn, qr = qs
        s = (jnp.einsum('bqhd,bkhd->bhqk', qn, k_nope)
             + jnp.einsum('bqhr,bkr->bhqk', qr, k_rope)).astype(jnp.float32) * scale
        s = jnp.where(chunk_mask(qp, k_pos), s, NEG)
        p = jax.nn.softmax(s, -1).astype(v.dtype)
        return jnp.einsum('bhqk,bkhd->bqhd', p, v)

    o = map_query_blocks(block, (q_nope, q_rope), q_pos)
    return o.reshape(b, -1, MLA_HEADS * MLA_V) @ w_o


def diff_project(x, w_qkv):
    b, t, _ = x.shape
    nq = DIFF_HEADS * 2 * DIFF_QK
    q, k, v = jnp.split(x @ w_qkv, [nq, 2 * nq], -1)
    return (q.reshape(b, t, DIFF_HEADS, 2 * DIFF_QK), k.reshape(b, t, DIFF_HEADS, 2 * DIFF_QK),
            v.reshape(b, t, DIFF_HEADS, DIFF_V))


def diff_attend(q, q_pos, k, v, k_pos, lam_q1, lam_k1, lam_q2, lam_k2, g_sub, w_o, lam_init):
    b = q.shape[0]
    lam = (jnp.exp(jnp.sum(lam_q1.astype(jnp.float32) * lam_k1.astype(jnp.float32)))
           - jnp.exp(jnp.sum(lam_q2.astype(jnp.float32) * lam_k2.astype(jnp.float32))) + lam_init)
    slopes = alibi_slopes(DIFF_HEADS)
    k1, k2 = k[..., :DIFF_QK], k[..., DIFF_QK:]
    scale = DIFF_QK ** -0.5

    def block(qs, qp):
        (qq,) = qs
        q1, q2 = qq[..., :DIFF_QK], qq[..., DIFF_QK:]
        dist = jnp.abs(qp[:, None] - k_pos[None, :]).astype(jnp.float32)
        bias = jnp.where(chunk_mask(qp, k_pos), -slopes[:, None, None] * dist, NEG)
        a1 = jax.nn.softmax(jnp.einsum('bqhd,bkhd->bhqk', q1, k1).astype(jnp.float32) * scale + bias, -1)
        a2 = jax.nn.softmax(jnp.einsum('bqhd,bkhd->bhqk', q2, k2).astype(jnp.float32) * scale + bias, -1)
        a = (a1 - lam * a2).astype(v.dtype)
        return jnp.einsum('bhqk,bkhd->bqhd', a, v)

    o = map_query_blocks(block, (q,), q_pos)
    o = rms_norm(o, g_sub) * (1.0 - lam_init)
    return o.reshape(b, -1, DIFF_HEADS * DIFF_V) @ w_o


def peer(x, w_query, sub_keys, u_tab, v_tab):
    b, t, d = x.shape
    n = b * t
    nblk = -(-n // PEER_BLOCK)
    xp = jnp.pad(x.reshape(n, d), ((0, nblk * PEER_BLOCK - n), (0, 0))).reshape(nblk, PEER_BLOCK, d)

    def block(xb):
        q = (xb @ w_query).reshape(PEER_BLOCK, PEER_HEADS, 2, PEER_DKEY // 2)
        s = jnp.einsum('nhcd,hckd->nhck', q, sub_keys)
        sv, si = lax.top_k(s, PEER_TOPK)
        cand = (sv[:, :, 0, :, None] + sv[:, :, 1, None, :]).reshape(PEER_BLOCK, PEER_HEADS, -1)
        cidx = (si[:, :, 0, :, None] * N_KEYS + si[:, :, 1, None, :]).reshape(PEER_BLOCK, PEER_HEADS, -1)
        fv, fi = lax.top_k(cand, PEER_TOPK)
        eidx = jnp.take_along_axis(cidx, fi, -1)
        g = jax.nn.softmax(fv.astype(jnp.float32), -1).astype(xb.dtype)
        h = jax.nn.gelu(jnp.einsum('nd,nhkd->nhk', xb, u_tab[eidx]))
        return jnp.einsum('nhk,nhkd->nd', g * h, v_tab[eidx])

    out = lax.map(block, xp).reshape(-1, d)[:n]
    return out.reshape(b, t, d)


def setup_inputs(seed: int = 0) -> dict:
    key = jax.random.key(seed)
    ks = iter(jax.random.split(key, 40))

    def nrm(shape, scale=1.0):
        return jax.random.normal(next(ks), shape, jnp.float32) * scale

    x_prompt = nrm((BATCH, SEQ, D_MODEL))
    x_sample = nrm((DEC_BATCH, DEC_SEQ, D_MODEL))
    cache_mla_ckv = nrm((N_MLA, DEC_BATCH, PAST_LEN, MLA_KV_LORA))
    cache_mla_krope = nrm((N_MLA, DEC_BATCH, PAST_LEN, MLA_ROPE))
    cache_diff_k = nrm((N_DIFF, DEC_BATCH, PAST_LEN, DIFF_HEADS, 2 * DIFF_QK))
    cache_diff_v = nrm((N_DIFF, DEC_BATCH, PAST_LEN, DIFF_HEADS, DIFF_V), BETA)

    mla_w_dqkv = nrm((N_MLA, D_MODEL, MLA_Q_LORA + MLA_KV_LORA + MLA_ROPE), D_MODEL ** -0.5)
    mla_g_q = 1.0 + nrm((N_MLA, MLA_Q_LORA), 0.01)
    mla_w_uq = nrm((N_MLA, MLA_Q_LORA, MLA_HEADS * (MLA_NOPE + MLA_ROPE)), MLA_Q_LORA ** -0.5)
    mla_g_kv = 1.0 + nrm((N_MLA, MLA_KV_LORA), 0.01)
    w_uk = nrm((N_MLA, MLA_KV_LORA, MLA_HEADS, MLA_NOPE), MLA_KV_LORA ** -0.5)
    w_uv = nrm((N_MLA, MLA_KV_LORA, MLA_HEADS, MLA_V), BETA * MLA_KV_LORA ** -0.5)
    mla_w_ukv = jnp.concatenate([w_uk, w_uv], -1).reshape(N_MLA, MLA_KV_LORA, MLA_HEADS * (MLA_NOPE + MLA_V))
    mla_w_o = nrm((N_MLA, MLA_HEADS * MLA_V, D_MODEL), BETA * (MLA_HEADS * MLA_V) ** -0.5)

    w_qk = nrm((N_DIFF, D_MODEL, 2 * DIFF_HEADS * 2 * DIFF_QK), D_MODEL ** -0.5)
    w_v = nrm((N_DIFF, D_MODEL, DIFF_HEADS * DIFF_V), BETA * D_MODEL ** -0.5)
    diff_w_qkv = jnp.concatenate([w_qk, w_v], -1)
    diff_lam_q1 = nrm((N_DIFF, DIFF_QK), 0.1)
    diff_lam_k1 = nrm((N_DIFF, DIFF_QK), 0.1)
    diff_lam_q2 = nrm((N_DIFF, DIFF_QK), 0.1)
    diff_lam_k2 = nrm((N_DIFF, DIFF_QK), 0.1)
    diff_g_sub = 1.0 + nrm((N_DIFF, DIFF_V), 0.01)
    diff_w_o = nrm((N_DIFF, DIFF_HEADS * DIFF_V, D_MODEL), BETA * (DIFF_HEADS * DIFF_V) ** -0.5)

    peer_w_query = nrm((DEPTH, D_MODEL, PEER_HEADS * PEER_DKEY), D_MODEL ** -0.5)
    peer_sub_keys = nrm((DEPTH, PEER_HEADS, 2, N_KEYS, PEER_DKEY // 2), (PEER_DKEY // 2) ** -0.5)
    peer_u = nrm((DEPTH, N_EXPERTS, D_MODEL), D_MODEL ** -0.5)
    peer_v = nrm((DEPTH, N_EXPERTS, D_MODEL), BETA * PEER_HEADS ** -0.5)

    ln_mix_g = 1.0 + nrm((DEPTH, D_MODEL), 0.01)
    ln_mix_b = nrm((DEPTH, D_MODEL), 0.01)
    ln_ffn_g = 1.0 + nrm((DEPTH, D_MODEL), 0.01)
    ln_ffn_b = nrm((DEPTH, D_MODEL), 0.01)

    return {"x_prompt": x_prompt, "x_sample": x_sample,
            "cache_mla_ckv": cache_mla_ckv, "cache_mla_krope": cache_mla_krope,
            "cache_diff_k": cache_diff_k, "cache_diff_v": cache_diff_v,
            "mla_w_dqkv": mla_w_dqkv, "mla_g_q": mla_g_q, "mla_w_uq": mla_w_uq, "mla_g_kv": mla_g_kv,
            "mla_w_ukv": mla_w_ukv, "mla_w_o": mla_w_o,
            "diff_w_qkv": diff_w_qkv, "diff_lam_q1": diff_lam_q1, "diff_lam_k1": diff_lam_k1,
            "diff_lam_q2": diff_lam_q2, "diff_lam_k2": diff_lam_k2, "diff_g_sub": diff_g_sub,
            "diff_w_o": diff_w_o,
            "peer_w_query": peer_w_query, "peer_sub_keys": peer_sub_keys, "peer_u": peer_u, "peer_v": peer_v,
            "ln_mix_g": ln_mix_g, "ln_mix_b": ln_mix_b, "ln_ffn_g": ln_ffn_g, "ln_ffn_b": ln_ffn_b}


def reference(x_prompt, x_sample, cache_mla_ckv, cache_mla_krope, cache_diff_k, cache_diff_v,
              mla_w_dqkv, mla_g_q, mla_w_uq, mla_g_kv, mla_w_ukv, mla_w_o,
              diff_w_qkv, diff_lam_q1, diff_lam_k1, diff_lam_q2, diff_lam_k2, diff_g_sub, diff_w_o,
              peer_w_query, peer_sub_keys, peer_u, peer_v,
              ln_mix_g, ln_mix_b, ln_ffn_g, ln_ffn_b):
    t_p = x_prompt.shape[1]
    t_s = x_sample.shape[1]
    past = cache_mla_ckv.shape[2]
    p_pos = jnp.arange(t_p)
    s_pos = past + jnp.arange(t_s)
    s_kpos = jnp.arange(past + t_s)

    xp, xs = x_prompt, x_sample
    p_ckv, p_kr, p_dk, p_dv = [], [], [], []
    s_ckv, s_kr, s_dk, s_dv = [], [], [], []
    for i in range(DEPTH):
        j = i // N_MIXERS
        if i % N_MIXERS == 0:
            qn, qr, ckv, kr = mla_project(xp, p_pos, mla_w_dqkv[j], mla_g_q[j], mla_w_uq[j], mla_g_kv[j])
            mix_p = mla_attend(qn, qr, p_pos, ckv, kr, p_pos, mla_w_ukv[j], mla_w_o[j])
            p_ckv.append(ckv)
            p_kr.append(kr)
            qn, qr, ckv, kr = mla_project(xs, s_pos, mla_w_dqkv[j], mla_g_q[j], mla_w_uq[j], mla_g_kv[j])
            mix_s = mla_attend(qn, qr, s_pos,
                               jnp.concatenate([cache_mla_ckv[j], ckv], 1),
                               jnp.concatenate([cache_mla_krope[j], kr], 1), s_kpos,
                               mla_w_ukv[j], mla_w_o[j])
            s_ckv.append(ckv)
            s_kr.append(kr)
        else:
            lam_init = 0.8 - 0.6 * math.exp(-0.3 * i)
            lam_args = (diff_lam_q1[j], diff_lam_k1[j], diff_lam_q2[j], diff_lam_k2[j], diff_g_sub[j], diff_w_o[j], lam_init)
            q, k, v = diff_project(xp, diff_w_qkv[j])
            mix_p = diff_attend(q, p_pos, k, v, p_pos, *lam_args)
            p_dk.append(k)
            p_dv.append(v)
            q, k, v = diff_project(xs, diff_w_qkv[j])
            mix_s = diff_attend(q, s_pos, jnp.concatenate([cache_diff_k[j], k], 1),
                                jnp.concatenate([cache_diff_v[j], v], 1), s_kpos, *lam_args)
            s_dk.append(k)
            s_dv.append(v)
        xp = layer_norm(ALPHA * xp + mix_p, ln_mix_g[i], ln_mix_b[i])
        xs = layer_norm(ALPHA * xs + mix_s, ln_mix_g[i], ln_mix_b[i])
        xp = layer_norm(ALPHA * xp + peer(xp, peer_w_query[i], peer_sub_keys[i], peer_u[i], peer_v[i]),
                        ln_ffn_g[i], ln_ffn_b[i])
        xs = layer_norm(ALPHA * xs + peer(xs, peer_w_query[i], peer_sub_keys[i], peer_u[i], peer_v[i]),
                        ln_ffn_g[i], ln_ffn_b[i])

    return (xp, xs, jnp.stack(p_ckv), jnp.stack(p_kr), jnp.stack(p_dk), jnp.stack(p_dv),
            jnp.stack(s_ckv), jnp.stack(s_kr), jnp.stack(s_dk), jnp.stack(s_dv))
```

```python
import math
import numpy as np
from contextlib import ExitStack
import concourse.bass as bass
import concourse.mybir as mybir
from concourse.bass_utils import run_bass_kernel_spmd

F32 = mybir.dt.float32
BF16 = mybir.dt.bfloat16
U32 = mybir.dt.uint32
AF = mybir.ActivationFunctionType
ALU = mybir.AluOpType
AX = mybir.AxisListType

NT, NPR, NSM, DM = 2080, 2048, 32, 2048
NKC = 3104
ALPHA = 4 ** 0.25
BETA = 16 ** -0.25
LN_EPS = 1e-5
RMS_EPS = 1e-6
TB = [(i * 128, 128) for i in range(16)] + [(2048, 32)]
TT = [(i * 512, 512) for i in range(4)] + [(2048, 32)]
TILE = 1040
LAM_INIT = 0.8 - 0.6 * math.exp(-0.3 * 1)


class Sync:
    def __init__(self, nc, es):
        self.nc = nc
        self.es = es
        self.eng = {'pe': nc.tensor, 'act': nc.scalar, 'dve': nc.vector, 'pool': nc.gpsimd, 'sp': nc.sync}
        self.semobj = {}
        self.ecnt = {}
        for k in self.eng:
            self.semobj['es_' + k] = es.enter_context(nc.semaphore('es_' + k))
            self.ecnt[k] = 0
        self.waited = {k: {} for k in self.eng}
        self.keys = {}
        self.dcnt = {}
        self.rotc = {}
        self.alias = {}
        self.free = []

    def rot(self, name, n):
        c = self.rotc.get(name, 0)
        self.rotc[name] = c + 1
        return c % n

    def _key(self, k):
        if k not in self.keys:
            self.keys[k] = {'w': None, 'r': {}}
        return self.keys[k]

    def _waits(self, en, reads, writes):
        need = {}

        def add(ev):
            if ev is None:
                return
            s, v = ev
            if need.get(s, 0) < v:
                need[s] = v
        for k in reads:
            add(self._key(k)['w'])
        for k in writes:
            kk = self._key(k)
            add(kk['w'])
            for ev in kk['r'].values():
                add(ev)
        e = self.eng[en]
        for sname, v in need.items():
            if en == 'pe' and sname == 'es_pe':
                continue
            if self.waited[en].get(sname, 0) >= v:
                continue
            e.wait_ge(self.semobj[sname], v)
            self.waited[en][sname] = v

    def _record(self, ev, reads, writes):
        for k in writes:
            kk = self._key(k)
            kk['w'] = ev
            kk['r'] = {}
        for k in reads:
            if k in writes:
                continue
            self._key(k)['r'][ev[0]] = ev

    def op(self, en, fn, reads=(), writes=()):
        self._waits(en, reads, writes)
        ins = fn(self.eng[en])
        self.ecnt[en] += 1
        sname = 'es_' + en
        ins.then_inc(self.semobj[sname], 1)
        self._record((sname, self.ecnt[en]), reads, writes)

    def dma(self, q, out, in_, reads=(), writes=(), sem=None):
        if sem is None:
            sem = 'ds_' + str(writes[0])
        can = self.alias.get(sem)
        if can is None:
            if self.free:
                can = self.free.pop()
            else:
                can = 'dq%d' % len(self.dcnt)
                self.semobj[can] = self.es.enter_context(self.nc.semaphore(can))
                self.dcnt[can] = 0
            self.alias[sem] = can
        self._waits(q, reads, writes)
        ins = self.eng[q].dma_start(out=out, in_=in_)
        self.dcnt[can] += 16
        ins.then_inc(self.semobj[can], 16)
        self._record((can, self.dcnt[can]), reads, writes)

    def barrier(self):
        for en, e in self.eng.items():
            for sname, c in self.dcnt.items():
                if c > 0 and self.waited[en].get(sname, 0) < c:
                    e.wait_ge(self.semobj[sname], c)
                    self.waited[en][sname] = c
            for k, c in self.ecnt.items():
                sname = 'es_' + k
                if c > 0 and k != en and self.waited[en].get(sname, 0) < c:
                    e.wait_ge(self.semobj[sname], c)
                    self.waited[en][sname] = c
        self.free = sorted(self.dcnt.keys(), reverse=True)
        self.alias = {}


_DECL = []
_NI1 = [128]


class _Stop(Exception):
    pass


def build_nc(stop=None, nheads0=16, nheads1=8, ni1=128):
    nc = bass.Bass("TRN2", target_bir_lowering=False)

    big = stop in (None, 'P0', 'DF', 'E1', 'R1') and ni1 > 0
    _DECL.clear()

    def din(name, shape, dt=F32):
        if name in ('p_uT', 'p_v') and not big:
            return None
        _DECL.append(name)
        return nc.dram_tensor(name, list(shape), dt, kind="ExternalInput").ap()

    def dout(name, shape):
        return nc.dram_tensor(name, list(shape), F32, kind="ExternalOutput").ap()

    x_all = din("x_all", [NT, DM])
    c_ckv = din("c_ckv", [1024, 512])
    c_kr = din("c_kr", [1024, 64])
    c_dk = din("c_dk", [1024, 2048])
    c_dv = din("c_dv", [1024, 2048])
    w_dqkv = din("w_dqkv", [2048, 1088])
    w_uq = din("w_uq", [512, 3072])
    w_uqs = din("w_uqs", [512, 1024])
    w_ukv = din("w_ukv", [512, 4096])
    w_o0 = din("w_o0", [2048, 2048])
    w_qkv = din("w_qkv", [2048, 6144])
    w_o1 = din("w_o1", [2048, 2048])
    p_wq = din("p_wq", [2, 16, 128, 16, 128])
    p_skT = din("p_skT", [2, 128, 16, 128])
    p_uT = din("p_uT", [2, max(ni1, 1), 128, 16, 128])
    p_v = din("p_v", [2, max(ni1, 1) * 128, 2048])
    _NI1[0] = ni1
    gq_bc = din("gq_bc", [128, 512])
    gkv_bc = din("gkv_bc", [128, 512])
    ln_g = din("ln_g", [4, 128, 2048])
    ln_b = din("ln_b", [4, 128, 2048])
    gsub = din("gsub", [128, 2])
    lamv = din("lamv", [128, 4, 128])
    c_ident = din("c_ident", [128, 128])
    c_iota = din("c_iota", [128, 128])
    c_iota16 = din("c_iota16", [128, 16])
    cosT = din("cosT", [NT, 32])
    sinT = din("sinT", [NT, 32])
    cosF = din("cosF", [64, NT])
    sinF = din("sinF", [64, NT])
    EBm = din("EBm", [128, 128])
    EBd = din("EBd", [128, 8, 128])
    biasTab = din("biasTab", [128, 8, 17])

    o_y = dout("o_y", [NT, DM])
    o_ckv = dout("o_ckv", [NT, 512])
    o_kr = dout("o_kr", [NT, 64])
    o_dk = dout("o_dk", [NT, 2048])
    o_dv = dout("o_dv", [NT, 2048])

    xa_d = nc.dram_tensor("xa_d", [NT, DM], F32, kind="Internal").ap()
    x1_d = nc.dram_tensor("x1_d", [NT, DM], F32, kind="Internal").ap()
    GT = nc.dram_tensor("GT", [2, 128, 128, TILE], BF16, kind="Internal").ap()
    ot_d = nc.dram_tensor("ot_d", [16, 128, NT], BF16, kind="Internal").ap()

    try:
      with ExitStack() as es0:
        S = Sync(nc, es0)

        def chk(name):
            if stop == name:
                S.barrier()
                src = {'E0': xa_d, 'P0': x1_d, 'E1': xa_d}.get(name)
                if src is not None:
                    for r0 in range(0, NT, 260):
                        S.dma('sp', o_y[r0:r0 + 260, :], src[r0:r0 + 260, :], writes=['dbg_oy'], sem='dbg_oy')
                    S.barrier()
                raise _Stop()

        uid = [0]

        def T(es, name, shape, dt):
            uid[0] += 1
            return es.enter_context(nc.sbuf_tensor("%s_%d" % (name, uid[0]), list(shape), dt))

        def P(es, name, shape, dt=F32):
            uid[0] += 1
            return es.enter_context(nc.psum_tensor("%s_%d" % (name, uid[0]), list(shape), dt))

        AT = T(es0, "AT", [128, 16, NT], BF16)
        ident = T(es0, "ident", [128, 128], F32)
        iotaF = T(es0, "iotaF", [128, 128], F32)
        iota16 = T(es0, "iota16", [128, 16], F32)
        ones_f = T(es0, "ones_f", [128, 128], F32)
        ones_b = T(es0, "ones_b", [128, 128], BF16)
        S.dma('sp', ident[:], c_ident[:, :], writes=['ident'])
        S.dma('sp', iotaF[:], c_iota[:, :], writes=['iotaF'])
        S.dma('sp', iota16[:], c_iota16[:, :], writes=['iota16'])
        S.op('dve', lambda e: e.memset(ones_f[:], 1.0), writes=['ones_f'])
        S.op('dve', lambda e: e.tensor_copy(ones_b[:], ones_f[:]), reads=['ones_f'], writes=['ones_b'])

        def atk(c0, n):
            ks = set()
            c = c0
            while c < c0 + n:
                ks.add(('AT', c // 16))
                c += 16
            return sorted(ks)

        def atk_blocks(c0, n):
            return [('ATB', b) for b in range(c0 // 16, (c0 + n + 15) // 16)]

        def ak(c0, n):
            return [('AT', g) for g in range(c0 // 16, (c0 + n - 1) // 16 + 1)]

        def transpose_to(es_tp, tp, src, n, nch, dst_ap_fn, rkeys, wkeys, np_=128):
            for ch in range(nch):
                S.op('pe', lambda e, ch=ch: e.transpose(tp[:np_, ch, :n], src[:n, ch * np_:(ch + 1) * np_], ident[:n, :n]),
                     reads=list(rkeys) + ['ident'], writes=['tp'])
            if nch >= 8:
                h = nch // 2
                S.op('act', lambda e: e.copy(dst_ap_fn(0, h), tp[:np_, 0:h, :n]), reads=['tp'], writes=list(wkeys) + ['_tpa'])
                S.op('dve', lambda e: e.tensor_copy(dst_ap_fn(h, nch), tp[:np_, h:nch, :n]), reads=['tp'], writes=list(wkeys) + ['_tpb'])
            else:
                S.op('act', lambda e: e.copy(dst_ap_fn(0, nch), tp[:np_, 0:nch, :n]), reads=['tp'], writes=list(wkeys))

        def layer_norm(y, n, gam, bet, out, ykey, okey):
            st = lnst
            for j in range(4):
                S.op('dve', lambda e, j=j: e.bn_stats(st[:n, j, :], y[:n, j * 512:(j + 1) * 512]), reads=[ykey], writes=['lnst%d' % j])
            S.op('dve', lambda e: e.bn_aggr(mv[:n, :], st[:n, :, :].rearrange("p a b -> p (a b)")),
                 reads=['lnst0', 'lnst1', 'lnst2', 'lnst3'], writes=['mv'])
            S.op('dve', lambda e: e.tensor_scalar(rs[:n, 0:1], mv[:n, 1:2], LN_EPS, None, op0=ALU.add), reads=['mv'], writes=['rs'])
            S.op('act', lambda e: e.activation(rs[:n, 0:1], rs[:n, 0:1], AF.Sqrt), reads=['rs'], writes=['rs'])
            S.op('dve', lambda e: e.reciprocal(rs[:n, 0:1], rs[:n, 0:1]), reads=['rs'], writes=['rs'])
            S.op('dve', lambda e: e.scalar_tensor_tensor(rs[:n, 1:2], mv[:n, 0:1], -1.0, rs[:n, 0:1], op0=ALU.mult, op1=ALU.mult),
                 reads=['rs', 'mv'], writes=['rs2'])
            S.op('act', lambda e: e.activation(out[:n, :], y[:n, :], AF.Identity, bias=rs[:n, 1:2], scale=rs[:n, 0:1]),
                 reads=[ykey, 'rs', 'rs2'], writes=[okey])
            S.op('pool', lambda e: e.tensor_tensor(out[:n, :], out[:n, :], gam[:n, :], op=ALU.mult), reads=[okey, 'lnp'], writes=[okey])
            S.op('pool', lambda e: e.tensor_tensor(out[:n, :], out[:n, :], bet[:n, :], op=ALU.add), reads=[okey, 'lnp'], writes=[okey])

        lnst = T(es0, "lnst", [128, 4, 6], F32)
        mv = T(es0, "mv", [128, 2], F32)
        rs = T(es0, "rs", [128, 4], F32)
        ss = T(es0, "ss", [128, 2], F32)

        def attention(es_a, pieces, V, nvc, groups, scale, hb, EB, psum, work, out_cb, tag):
            st_ps, oacc, sacc = psum
            pt, ptf, rc, On = work
            for (qc0, qblocks, kblocks) in groups:
                nq = sum(w for _, w in qblocks)
                nkb = len(kblocks)
                bsel = {}

                def emit_scores(idx):
                    (kc0, nk, vblk, i0, deltas) = kblocks[idx]
                    b = S.rot('st' + tag, 2)
                    bsel[idx] = b
                    o0 = qblocks[i0][0]
                    ncol = nq - o0
                    for pi, (kT, kkeys, qT, qkeys, Kp) in enumerate(pieces):
                        S.op('pe', lambda e, kT=kT, qT=qT, Kp=Kp, pi=pi: e.matmul(
                            st_ps[:nk, b, o0:o0 + ncol], lhsT=kT[:Kp, kc0:kc0 + nk], rhs=qT[:Kp, qc0 + o0:qc0 + o0 + ncol],
                            start=(pi == 0), stop=(pi == len(pieces) - 1)),
                            reads=list(kkeys) + list(qkeys), writes=['st%d' % b])

                emit_scores(0)
                for idx, (kc0, nk, vblk, i0, deltas) in enumerate(kblocks):
                    if idx + 1 < nkb:
                        emit_scores(idx + 1)
                    b = bsel[idx]
                    o0 = qblocks[i0][0]
                    ncol = nq - o0
                    for j, dl in enumerate(deltas):
                        qo, qw = qblocks[i0 + j]
                        if dl > 0:
                            bias = biasTab_sb[:nk, hb, dl:dl + 1] if hb is not None else 0.0
                            S.op('act', lambda e, qo=qo, qw=qw, bias=bias: e.activation(
                                pt[:nk, b, qo:qo + qw], st_ps[:nk, b, qo:qo + qw], AF.Exp, bias=bias, scale=scale),
                                reads=['st%d' % b, 'btab'], writes=[('pt', b, j + i0)])
                        else:
                            S.op('act', lambda e, qo=qo, qw=qw: e.activation(
                                ptf[:nk, :qw], st_ps[:nk, b, qo:qo + qw], AF.Exp, scale=scale),
                                reads=['st%d' % b], writes=['ptf'])
                            S.op('dve', lambda e, qo=qo, qw=qw: e.tensor_tensor(
                                pt[:nk, b, qo:qo + qw], ptf[:nk, :qw], EB[:nk, :qw], op=ALU.mult),
                                reads=['ptf', 'EB'], writes=[('pt', b, j + i0)])
                    ptk = [('pt', b, j) for j in range(i0, len(qblocks))]
                    for vc in range(nvc):
                        S.op('pe', lambda e, vc=vc: e.matmul(
                            oacc[:, vc, o0:o0 + ncol], lhsT=V[:nk, vblk, vc * 128:(vc + 1) * 128], rhs=pt[:nk, b, o0:o0 + ncol],
                            start=(idx == 0), stop=(idx == nkb - 1), skip_group_check=True),
                            reads=ptk + ['V' + tag], writes=['oacc'])
                    S.op('pe', lambda e: e.matmul(
                        sacc[:, o0:o0 + ncol], lhsT=ones_b[:nk, :], rhs=pt[:nk, b, o0:o0 + ncol],
                        start=(idx == 0), stop=(idx == nkb - 1), skip_group_check=True),
                        reads=ptk + ['ones_b'], writes=['sacc'])
                S.op('dve', lambda e: e.reciprocal(rc[:, :nq], sacc[:, :nq]), reads=['sacc'], writes=['rc'])
                for vc in range(nvc):
                    S.op('dve', lambda e, vc=vc: e.tensor_tensor(On[:, vc, :nq], oacc[:, vc, :nq], rc[:, :nq], op=ALU.mult),
                         reads=['oacc', 'rc'], writes=[('On', vc)])
                out_cb(qc0, nq)

        def prompt_groups():
            gs = []
            for g in range(4):
                qblocks = [(i * 128, 128) for i in range(4)]
                kbl = []
                for kb in range(4 * g + 4):
                    i0 = max(0, kb - 4 * g)
                    kbl.append((kb * 128, 128, kb, i0, [4 * g + i - kb for i in range(i0, 4)]))
                gs.append((g * 512, qblocks, kbl))
            return gs

        def sample_groups():
            kbl = [(2080 + 128 * j, 128, 17 + j, 0, [8 - j]) for j in range(8)]
            kbl.append((2048, 32, 16, 0, [0]))
            return [(2048, [(0, 32)], kbl)]

        with ExitStack() as es:
            xblk = T(es, "xblkA", [128, 2, 2048], F32)
            tp = P(es, "tpA", [128, 16, 128])
            for (r0, n) in TB:
                b = S.rot('xblk', 2)
                S.dma('sp', xblk[:n, b, :], x_all[r0:r0 + n, :], writes=['xblk%d' % b])
                transpose_to(es, tp, xblk[:, b, :], n, 16, lambda a, c, r0=r0, n=n: AT[:, a:c, r0:r0 + n],
                             ['xblk%d' % b], ak(r0, n))
            S.barrier()
        chk('A')

        esM = es0.enter_context(ExitStack())
        cqnT = T(esM, "cqnT", [128, 4, NT], BF16)
        ckvnT = T(esM, "ckvnT", [128, 4, NKC], BF16)
        krT = T(esM, "krT", [64, NKC], BF16)
        with ExitStack() as es:
            Wd = T(es, "Wd", [128, 16, 1088], BF16)
            stg = T(es, "stgB", [128, 2, 1088], F32)
            gq = T(es, "gq", [128, 512], F32)
            gkv = T(es, "gkv", [128, 512], F32)
            cs_t = T(es, "cs_t", [128, 2, 64], F32)
            junk = T(es, "junk", [128, 512], F32)
            cqn = T(es, "cqn", [128, 512], F32)
            ckvn = T(es, "ckvn", [128, 2, 512], F32)
            krs = T(es, "krs", [128, 64], F32)
            kro = T(es, "kro", [128, 2, 64], F32)
            tmp4 = T(es, "tmp4", [128, 4, 32], F32)
            cblk = T(es, "cblk", [128, 2, 576], F32)
            lat = P(es, "lat", [128, 3, 512])
            tp = P(es, "tpB", [128, 16, 128])
            S.dma('sp', gq[:], gq_bc[:, :], writes=['gq'])
            S.dma('sp', gkv[:], gkv_bc[:, :], writes=['gkv'])
            for dc in range(16):
                b = S.rot('stgB', 2)
                S.dma('sp', stg[:, b, :], w_dqkv[dc * 128:(dc + 1) * 128, :], writes=['stgB%d' % b])
                S.op('pool', lambda e, dc=dc, b=b: e.tensor_copy(Wd[:, dc, :], stg[:, b, :]), reads=['stgB%d' % b], writes=['Wd'])
            for (r0, n) in TB:
                for j, (c0, w) in enumerate([(0, 512), (512, 512), (1024, 64)]):
                    for dc in range(16):
                        S.op('pe', lambda e, j=j, c0=c0, w=w, dc=dc: e.matmul(
                            lat[:n, j, :w], lhsT=AT[:, dc, r0:r0 + n], rhs=Wd[:, dc, c0:c0 + w], start=(dc == 0), stop=(dc == 15)),
                            reads=ak(r0, n) + ['Wd'], writes=['lat%d' % j])
                bb = S.rot('ckvn', 2)
                S.dma('sp', cs_t[:n, 0, 0:32], cosT[r0:r0 + n, :], writes=['cs_a'])
                S.dma('sp', cs_t[:n, 1, 0:32], sinT[r0:r0 + n, :], writes=['cs_b'])
                S.op('act', lambda e: e.activation(junk[:n, :], lat[:n, 0, :], AF.Square, accum_out=ss[:n, 0:1]), reads=['lat0'], writes=['junk', 'ss0'])
                S.op('act', lambda e: e.activation(junk[:n, :], lat[:n, 1, :], AF.Square, accum_out=ss[:n, 1:2]), reads=['lat1'], writes=['junk', 'ss1'])
                S.op('dve', lambda e: e.tensor_scalar(rs[:n, 0:2], ss[:n, 0:2], 1.0 / 512, RMS_EPS, op0=ALU.mult, op1=ALU.add), reads=['ss0', 'ss1'], writes=['rs'])
                S.op('act', lambda e: e.activation(rs[:n, 0:2], rs[:n, 0:2], AF.Sqrt), reads=['rs'], writes=['rs'])
                S.op('dve', lambda e: e.reciprocal(rs[:n, 0:2], rs[:n, 0:2]), reads=['rs'], writes=['rs'])
                S.op('dve', lambda e: e.scalar_tensor_tensor(cqn[:n, :], lat[:n, 0, :], rs[:n, 0:1], gq[:n, :], op0=ALU.mult, op1=ALU.mult),
                     reads=['lat0', 'rs', 'gq'], writes=['cqn'])
                S.op('dve', lambda e: e.scalar_tensor_tensor(ckvn[:n, bb, :], lat[:n, 1, :], rs[:n, 1:2], gkv[:n, :], op0=ALU.mult, op1=ALU.mult),
                     reads=['lat1', 'rs', 'gkv'], writes=['ckvn%d' % bb])
                S.dma('sp', o_ckv[r0:r0 + n, :], ckvn[:n, bb, :], reads=['ckvn%d' % bb], writes=['o_ckv'], sem='st_ckvn%d' % bb)
                S.op('act', lambda e: e.copy(krs[:n, :], lat[:n, 2, 0:64]), reads=['lat2'], writes=['krs'])
                S.op('dve', lambda e: e.tensor_tensor(tmp4[:n, 0, :], krs[:n, 0:32], cs_t[:n, 0, 0:32], op=ALU.mult), reads=['krs', 'cs_a'], writes=['t40'])
                S.op('dve', lambda e: e.tensor_tensor(tmp4[:n, 1, :], krs[:n, 32:64], cs_t[:n, 1, 0:32], op=ALU.mult), reads=['krs', 'cs_b'], writes=['t41'])
                S.op('dve', lambda e: e.tensor_tensor(tmp4[:n, 2, :], krs[:n, 32:64], cs_t[:n, 0, 0:32], op=ALU.mult), reads=['krs', 'cs_a'], writes=['t42'])
                S.op('dve', lambda e: e.tensor_tensor(tmp4[:n, 3, :], krs[:n, 0:32], cs_t[:n, 1, 0:32], op=ALU.mult), reads=['krs', 'cs_b'], writes=['t43'])
                S.op('dve', lambda e: e.tensor_tensor(kro[:n, bb, 0:32], tmp4[:n, 0, :], tmp4[:n, 1, :], op=ALU.subtract), reads=['t40', 't41'], writes=['kro%da' % bb])
                S.op('dve', lambda e: e.tensor_tensor(kro[:n, bb, 32:64], tmp4[:n, 2, :], tmp4[:n, 3, :], op=ALU.add), reads=['t42', 't43'], writes=['kro%db' % bb])
                S.dma('sp', o_kr[r0:r0 + n, :], kro[:n, bb, :], reads=['kro%da' % bb, 'kro%db' % bb], writes=['o_kr'], sem='st_kro%d' % bb)
                transpose_to(es, tp, cqn, n, 4, lambda a, c: cqnT[:, a:c, r0:r0 + n], ['cqn'], ['cqnT'])
                transpose_to(es, tp, ckvn[:, bb, :], n, 4, lambda a, c: ckvnT[:, a:c, r0:r0 + n], ['ckvn%d' % bb], ['ckvnT'])
                transpose_to(es, tp, kro[:, bb, :], n, 1, lambda a, c: krT[:64, r0:r0 + n].unsqueeze(1), ['kro%da' % bb, 'kro%db' % bb], ['krT'], np_=64)
            for j in range(8):
                b = S.rot('cblk', 2)
                S.dma('sp', cblk[:, b, 0:512], c_ckv[j * 128:(j + 1) * 128, :], writes=['cblk%d' % b])
                S.dma('sp', cblk[:, b, 512:576], c_kr[j * 128:(j + 1) * 128, :], writes=['cblk%d' % b])
                c0 = 2080 + j * 128
                transpose_to(es, tp, cblk[:, b, 0:512], 128, 4, lambda a, c: ckvnT[:, a:c, c0:c0 + 128], ['cblk%d' % b], ['ckvnT'])
                transpose_to(es, tp, cblk[:, b, 512:576], 128, 1, lambda a, c: krT[:64, c0:c0 + 128].unsqueeze(1), ['cblk%d' % b], ['krT'], np_=64)
            S.barrier()
        chk('B')

        with ExitStack() as es:
            stg = T(es, "stgC", [128, 4, 512], F32)
            wq_h = T(es, "wq_h", [128, 4, 192], BF16)
            wrs_h = T(es, "wrs_h", [128, 4, 64], BF16)
            wkv_h = T(es, "wkv_h", [128, 4, 256], BF16)
            cosF_sb = T(es, "cosF_sb", [64, NT], F32)
            sinF_sb = T(es, "sinF_sb", [64, NT], F32)
            EB_sb = T(es, "EBm_sb", [128, 128], F32)
            qnT = T(es, "qnT", [128, NT], BF16)
            qrT = T(es, "qrT", [64, NT], BF16)
            knT = T(es, "knT", [128, NKC], BF16)
            Vh = T(es, "Vh", [128, 25, 128], BF16)
            tm1 = T(es, "tm1", [64, 512], F32)
            tm2 = T(es, "tm2", [64, 512], F32)
            pt = T(es, "pt", [128, 2, 512], BF16)
            ptf = T(es, "ptf", [128, 128], F32)
            rc = T(es, "rc", [128, 512], F32)
            On = T(es, "On", [128, 2, 512], F32)
            st_ps = P(es, "st_ps", [128, 2, 512])
            oacc = P(es, "oacc", [128, 1, 512])
            sacc = P(es, "sacc", [128, 512])
            pq = P(es, "pq", [128, 512])
            pr = P(es, "pr", [64, 512])
            prs = P(es, "prs", [64, 512])
            S.dma('sp', cosF_sb[:], cosF[:, :], writes=['cosF'])
            S.dma('sp', sinF_sb[:], sinF[:, :], writes=['sinF'])
            S.dma('sp', EB_sb[:], EBm[:, :], writes=['EB'])
            biasTab_sb = None
            KT = [(i * 512, 512) for i in range(6)] + [(3072, 32)]
            VB = [(i * 128, 128) for i in range(16)] + [(2048, 32)] + [(2080 + 128 * j, 128) for j in range(8)]
            for h in range(nheads0):
                S.dma('sp', stg[:, :, 0:192], w_uq.rearrange("(c p) n -> p c n", p=128)[:, :, h * 192:(h + 1) * 192], writes=['stgC0'])
                S.op('pool', lambda e: e.tensor_copy(wq_h[:], stg[:, :, 0:192]), reads=['stgC0'], writes=['wq_h'])
                S.dma('sp', stg[:, :, 192:256], w_uqs.rearrange("(c p) n -> p c n", p=128)[:, :, h * 64:(h + 1) * 64], writes=['stgC1'])
                S.op('pool', lambda e: e.tensor_copy(wrs_h[:], stg[:, :, 192:256]), reads=['stgC1'], writes=['wrs_h'])
                S.dma('sp', stg[:, :, 256:512], w_ukv.rearrange("(c p) n -> p c n", p=128)[:, :, h * 256:(h + 1) * 256], writes=['stgC2'])
                S.op('pool', lambda e: e.tensor_copy(wkv_h[:], stg[:, :, 256:512]), reads=['stgC2'], writes=['wkv_h'])
                for (t0, n) in TT:
                    for ch in range(4):
                        S.op('pe', lambda e, ch=ch: e.matmul(pq[:, :n], lhsT=wq_h[:, ch, 0:128], rhs=cqnT[:, ch, t0:t0 + n], start=(ch == 0), stop=(ch == 3)),
                             reads=['wq_h', 'cqnT'], writes=['pq'])
                    S.op('act', lambda e: e.copy(qnT[:, t0:t0 + n], pq[:, :n]), reads=['pq'], writes=['qnT'])
                    for ch in range(4):
                        S.op('pe', lambda e, ch=ch: e.matmul(pr[:, :n], lhsT=wq_h[:, ch, 128:192], rhs=cqnT[:, ch, t0:t0 + n], start=(ch == 0), stop=(ch == 3)),
                             reads=['wq_h', 'cqnT'], writes=['pr'])
                    for ch in range(4):
                        S.op('pe', lambda e, ch=ch: e.matmul(prs[:, :n], lhsT=wrs_h[:, ch, :], rhs=cqnT[:, ch, t0:t0 + n], start=(ch == 0), stop=(ch == 3)),
                             reads=['wrs_h', 'cqnT'], writes=['prs'])
                    S.op('dve', lambda e: e.tensor_tensor(tm1[:, :n], pr[:, :n], cosF_sb[:, t0:t0 + n], op=ALU.mult), reads=['pr', 'cosF'], writes=['tm1'])
                    S.op('dve', lambda e: e.tensor_tensor(tm2[:, :n], prs[:, :n], sinF_sb[:, t0:t0 + n], op=ALU.mult), reads=['prs', 'sinF'], writes=['tm2'])
                    S.op('pool', lambda e: e.tensor_tensor(qrT[:, t0:t0 + n], tm1[:, :n], tm2[:, :n], op=ALU.add), reads=['tm1', 'tm2'], writes=['qrT'])
                for (t0, n) in KT:
                    for ch in range(4):
                        S.op('pe', lambda e, ch=ch: e.matmul(pq[:, :n], lhsT=wkv_h[:, ch, 0:128], rhs=ckvnT[:, ch, t0:t0 + n], start=(ch == 0), stop=(ch == 3)),
                             reads=['wkv_h', 'ckvnT'], writes=['pq'])
                    S.op('act', lambda e: e.copy(knT[:, t0:t0 + n], pq[:, :n]), reads=['pq'], writes=['knT'])
                for bi, (k0, nk) in enumerate(VB):
                    for ch in range(4):
                        S.op('pe', lambda e, ch=ch: e.matmul(pq[:nk, 0:128], lhsT=ckvnT[:, ch, k0:k0 + nk], rhs=wkv_h[:, ch, 128:256], start=(ch == 0), stop=(ch == 3)),
                             reads=['wkv_h', 'ckvnT'], writes=['pq'])
                    S.op('dve', lambda e: e.tensor_copy(Vh[:nk, bi, :], pq[:nk, 0:128]), reads=['pq'], writes=['Vm'])

                def cb(qc0, nq, h=h):
                    S.op('act', lambda e: e.copy(AT[:, h, qc0:qc0 + nq], On[:, 0, :nq]), reads=[('On', 0)], writes=ak(qc0, nq))
                pieces = [(knT, ['knT'], qnT, ['qnT'], 128), (krT, ['krT'], qrT, ['qrT'], 64)]
                attention(es, pieces, Vh, 1, prompt_groups() + sample_groups(), 192 ** -0.5, None, EB_sb,
                          (st_ps, oacc, sacc), (pt, ptf, rc, On), cb, 'm')
            S.barrier()
        esM.close()
        chk('CD')

        def wo_ln(w_o, res_src, lni, dst_d):
            with ExitStack() as es:
                Wo = T(es, "Wo", [128, 16, 2048], BF16)
                stg = T(es, "stgW", [128, 2, 2048], F32)
                gam = T(es, "gam", [128, 2048], F32)
                bet = T(es, "bet", [128, 2048], F32)
                xblk = T(es, "xblkW", [128, 2048], F32)
                y = T(es, "yW", [128, 2048], F32)
                xo = T(es, "xoW", [128, 2048], F32)
                mix = P(es, "mix", [128, 4, 512])
                tp = P(es, "tpW", [128, 16, 128])
                S.dma('sp', gam[:], ln_g[lni], writes=['lnp'])
                S.dma('sp', bet[:], ln_b[lni], writes=['lnp'])
                for ch in range(16):
                    b = S.rot('stgW', 2)
                    S.dma('sp', stg[:, b, :], w_o[ch * 128:(ch + 1) * 128, :], writes=['stgW%d' % b])
                    S.op('pool', lambda e, ch=ch, b=b: e.tensor_copy(Wo[:, ch, :], stg[:, b, :]), reads=['stgW%d' % b], writes=['Wo'])
                for (r0, n) in TB:
                    S.dma('sp', xblk[:n, :], res_src[r0:r0 + n, :], writes=['xblkW'])
                    for cs in range(4):
                        for ch in range(16):
                            S.op('pe', lambda e, cs=cs, ch=ch: e.matmul(mix[:n, cs, :], lhsT=AT[:, ch, r0:r0 + n], rhs=Wo[:, ch, cs * 512:(cs + 1) * 512],
                                                                      start=(ch == 0), stop=(ch == 15)),
                                 reads=ak(r0, n) + ['Wo'], writes=['mix%d' % cs])
                    S.op('dve', lambda e: e.scalar_tensor_tensor(y[:n, :], xblk[:n, :], ALPHA, mix[:n, :, :].rearrange("p a b -> p (a b)"),
                                                                 op0=ALU.mult, op1=ALU.add),
                         reads=['xblkW', 'mix0', 'mix1', 'mix2', 'mix3'], writes=['y'])
                    layer_norm(y, n, gam, bet, xo, 'y', 'xo')
                    S.dma('sp', dst_d[r0:r0 + n, :], xo[:n, :], reads=['xo'], writes=[('xa_d', r0)], sem='st_xoW')
                    transpose_to(es, tp, xo, n, 16, lambda a, c, r0=r0, n=n: AT[:, a:c, r0:r0 + n], ['xo'], ak(r0, n))
                S.barrier()

        def peer(l, lni, dst_d, final_out):
            with ExitStack() as es:
                stg = T(es, "stgQ", [128, 2, 16, 128], F32)
                wqc = T(es, "wqc", [128, 2, 16, 128], BF16)
                qT = T(es, "qTp", [128, 2, 16, 512], BF16)
                SKT = T(es, "SKT", [128, 16, 128], BF16)
                skf = stg[:, 0, :, :]
                Sst = T(es, "Sst", [128, 1, 128, 128], BF16)
                sv = T(es, "sv", [128, 16, 16], F32)
                si = T(es, "si", [128, 16, 16], U32)
                sif = T(es, "sif", [128, 16, 16], F32)
                scw = T(es, "scw", [128, 2, 128], F32)
                cand = T(es, "cand", [128, 8, 256], F32)
                cand2 = T(es, "cand2", [128, 8, 256], F32)
                fv = T(es, "fv", [128, 8, 16], F32)
                fi = T(es, "fi", [128, 8, 16], U32)
                fa = T(es, "fa", [128, 8, 16], U32)
                fb = T(es, "fb", [128, 8, 16], U32)
                faf = T(es, "faf", [128, 8, 16], F32)
                fbf = T(es, "fbf", [128, 8, 16], F32)
                I1 = T(es, "I1", [128, 128], F32)
                I2 = T(es, "I2", [128, 128], F32)
                gg = T(es, "gg", [128, 128], F32)
                ef = T(es, "ef", [128, 8, 16], F32)
                zz = T(es, "zz", [128, 8], F32)
                IT = T(es, "IT", [128, 3, 128], F32)
                An = T(es, "An", [128, 4, 128], BF16)
                Bn = T(es, "Bn", [128, 4, 128], BF16)
                pq = P(es, "pqR", [128, 512])
                sc = P(es, "sc", [128, 16, 128])
                tp = P(es, "tpR", [128, 3, 128])
                gp = P(es, "gp", [128, 2, 4, 128])
                S.dma('sp', skf, p_skT[l], writes=['stgQ0'])
                S.op('pool', lambda e: e.tensor_copy(SKT[:], skf), reads=['stgQ0'], writes=['SKT'])
                def qproj_cb(ti, cb):
                    (t0q, ntq) = TT[ti]
                    qb = ti % 2
                    b = S.rot('stgQ', 2)
                    S.dma('sp', stg[:, b, :, :], p_wq[l, cb], writes=['stgQ%d' % b])
                    S.op('act', lambda e: e.copy(wqc[:, b, :, :], stg[:, b, :, :]), reads=['stgQ%d' % b], writes=['wqc%d' % b])
                    for dc in range(16):
                        S.op('pe', lambda e, dc=dc: e.matmul(pq[:, :ntq], lhsT=wqc[:, b, dc, :], rhs=AT[:, dc, t0q:t0q + ntq], start=(dc == 0), stop=(dc == 15)),
                             reads=ak(t0q, ntq) + ['wqc%d' % b], writes=['pqR'])
                    S.op('act', lambda e: e.copy(qT[:, qb, cb, :ntq], pq[:, :ntq]), reads=['pqR'], writes=[('qTp', qb, cb)])

                for cb in range(16):
                    qproj_cb(0, cb)
                for ti, (t0, nt) in enumerate(TT):
                    qb = ti % 2
                    pending = list(range(16)) if ti + 1 < len(TT) else []
                    for bo in range(0, nt, 128):
                        n = min(128, nt - bo)
                        r0 = t0 + bo
                        sb_ = 0
                        for cb in range(16):
                            S.op('pe', lambda e, cb=cb: e.matmul(sc[:n, cb, :], lhsT=qT[:, qb, cb, bo:bo + n], rhs=SKT[:, cb, :], start=True, stop=True),
                                 reads=[('qTp', qb, cb), 'SKT'], writes=['sc'])
                        for cb0 in range(0, 16, 2):
                            for step in range(5):
                                for cb in (cb0, cb0 + 1):
                                    w_ = cb % 2
                                    if step == 0:
                                        S.op('dve', lambda e, cb=cb: e.max(sv[:n, cb, 0:8], sc[:n, cb, :]), reads=['sc'], writes=[('sva', cb)])
                                    elif step == 1:
                                        S.op('dve', lambda e, cb=cb: e.max_index(si[:n, cb, 0:8], sv[:n, cb, 0:8], sc[:n, cb, :]), reads=['sc', ('sva', cb)], writes=[('sia', cb)])
                                    elif step == 2:
                                        S.op('dve', lambda e, cb=cb: e.match_replace(scw[:n, w_, :], sv[:n, cb, 0:8], sc[:n, cb, :], -1e30), reads=['sc', ('sva', cb)], writes=[('scw', w_)])
                                    elif step == 3:
                                        S.op('dve', lambda e, cb=cb: e.max(sv[:n, cb, 8:16], scw[:n, w_, :]), reads=[('scw', w_)], writes=[('svb', cb)])
                                    else:
                                        S.op('dve', lambda e, cb=cb: e.max_index(si[:n, cb, 8:16], sv[:n, cb, 8:16], scw[:n, w_, :]), reads=[('scw', w_), ('svb', cb)], writes=[('sib', cb)])
                        svk = [('sva', c_) for c_ in range(16)] + [('svb', c_) for c_ in range(16)]
                        sik = [('sia', c_) for c_ in range(16)] + [('sib', c_) for c_ in range(16)]
                        S.op('dve', lambda e: e.tensor_copy(sif[:n], si[:n]), reads=sik, writes=['sif'])
                        sv4 = sv[:n].rearrange("p (h c) k -> p h c k", c=2)
                        sif4 = sif[:n].rearrange("p (h c) k -> p h c k", c=2)
                        c4 = cand[:n].rearrange("p h (a b) -> p h a b", b=16)
                        c24 = cand2[:n].rearrange("p h (a b) -> p h a b", b=16)
                        S.op('dve', lambda e: e.tensor_tensor(c4, sv4[:, :, 0, :].unsqueeze(3).to_broadcast([n, 8, 16, 16]),
                                                              sv4[:, :, 1, :].unsqueeze(2).to_broadcast([n, 8, 16, 16]), op=ALU.add),
                             reads=svk, writes=['cand'])
                        for h0 in range(0, 8, 2):
                            for step in range(5):
                                for h in (h0, h0 + 1):
                                    if step == 0:
                                        S.op('dve', lambda e, h=h: e.max(fv[:n, h, 0:8], cand[:n, h, :]), reads=['cand'], writes=[('fva', h)])
                                    elif step == 1:
                                        S.op('dve', lambda e, h=h: e.max_index(fi[:n, h, 0:8], fv[:n, h, 0:8], cand[:n, h, :]), reads=['cand', ('fva', h)], writes=[('fia', h)])
                                    elif step == 2:
                                        S.op('dve', lambda e, h=h: e.match_replace(cand2[:n, h, :], fv[:n, h, 0:8], cand[:n, h, :], -1e30), reads=['cand', ('fva', h)], writes=[('cand2', h)])
                                    elif step == 3:
                                        S.op('dve', lambda e, h=h: e.max(fv[:n, h, 8:16], cand2[:n, h, :]), reads=[('cand2', h)], writes=[('fvb', h)])
                                    else:
                                        S.op('dve', lambda e, h=h: e.max_index(fi[:n, h, 8:16], fv[:n, h, 8:16], cand2[:n, h, :]), reads=[('cand2', h), ('fvb', h)], writes=[('fib', h)])
                        fvk = [('fva', h_) for h_ in range(8)] + [('fvb', h_) for h_ in range(8)]
                        fik = [('fia', h_) for h_ in range(8)] + [('fib', h_) for h_ in range(8)]
                        c2k = [('cand2', h_) for h_ in range(8)]
                        S.op('dve', lambda e: e.tensor_single_scalar(fa[:n], fi[:n], 4, op=ALU.logical_shift_right), reads=fik, writes=['fa'])
                        S.op('dve', lambda e: e.tensor_single_scalar(fb[:n], fi[:n], 15, op=ALU.bitwise_and), reads=fik, writes=['fb'])
                        S.op('dve', lambda e: e.tensor_copy(faf[:n], fa[:n]), reads=['fa'], writes=['faf'])
                        S.op('dve', lambda e: e.tensor_copy(fbf[:n], fb[:n]), reads=['fb'], writes=['fbf'])
                        io4 = iota16[:n, :].unsqueeze(1).unsqueeze(1).to_broadcast([n, 8, 16, 16])
                        for (idxf, cc, dst, kk) in ((faf, 0, I1, 'I1'), (fbf, 1, I2, 'I2')):
                            S.op('dve', lambda e, idxf=idxf: e.tensor_tensor(c4, io4, idxf[:n].unsqueeze(3).to_broadcast([n, 8, 16, 16]), op=ALU.is_equal),
                                 reads=['iota16', 'faf', 'fbf'], writes=['cand'])
                            S.op('dve', lambda e, cc=cc: e.tensor_tensor(c24, c4, sif4[:, :, cc, :].unsqueeze(2).to_broadcast([n, 8, 16, 16]), op=ALU.mult),
                                 reads=['cand', 'sif'], writes=c2k + ['cand2'])
                            S.op('dve', lambda e, dst=dst: e.tensor_reduce(dst[:n, :].rearrange("p (h k) -> p h k", k=16), c24, axis=AX.X, op=ALU.add),
                                 reads=c2k + ['cand2'], writes=[kk])
                        S.op('dve', lambda e: e.tensor_tensor(ef[:n], fv[:n], fv[:n, :, 0:1].to_broadcast([n, 8, 16]), op=ALU.subtract), reads=fvk, writes=['ef'])
                        S.op('act', lambda e: e.activation(ef[:n], ef[:n], AF.Exp), reads=['ef'], writes=['ef'])
                        S.op('dve', lambda e: e.tensor_reduce(zz[:n, :], ef[:n], axis=AX.X, op=ALU.add), reads=['ef'], writes=['zz'])
                        S.op('dve', lambda e: e.reciprocal(zz[:n, :], zz[:n, :]), reads=['zz'], writes=['zz'])
                        S.op('dve', lambda e: e.tensor_tensor(gg[:n, :].rearrange("p (h k) -> p h k", k=16), ef[:n], zz[:n, :].unsqueeze(2).to_broadcast([n, 8, 16]), op=ALU.mult),
                             reads=['ef', 'zz'], writes=['gg'])
                        for j, (src, kk) in enumerate(((I1, 'I1'), (I2, 'I2'), (gg, 'gg'))):
                            S.op('pe', lambda e, j=j, src=src: e.transpose(tp[:, j, :n], src[:n, :], ident[:n, :n]), reads=[kk, 'ident'], writes=['tpR'])
                        S.op('act', lambda e: e.copy(IT[:, :, :n], tp[:, :, :n]), reads=['tpR'], writes=['IT'])
                        for tk in range(n):
                            q = S.rot('AB', 4)
                            S.op('dve', lambda e, q=q, tk=tk: e.tensor_scalar(An[:, q, :], iotaF[:, :], IT[:, 0, tk:tk + 1], IT[:, 2, tk:tk + 1],
                                                                             op0=ALU.is_equal, op1=ALU.mult),
                                 reads=['IT', 'iotaF'], writes=['An%d' % q])
                            S.op('dve', lambda e, q=q, tk=tk: e.tensor_scalar(Bn[:, q, :], iotaF[:, :], IT[:, 1, tk:tk + 1], None, op0=ALU.is_equal),
                                 reads=['IT', 'iotaF'], writes=['Bn%d' % q])
                            gb = (tk // 4) % 2
                            S.op('pe', lambda e, q=q, tk=tk, gb=gb: e.matmul(gp[:, gb, tk % 4, :], lhsT=Bn[:, q, :], rhs=An[:, q, :], start=True, stop=True),
                                 reads=['An%d' % q, 'Bn%d' % q], writes=['gp%d' % gb])
                            if tk % 4 == 3 or tk == n - 1:
                                m = tk % 4 + 1
                                tk0 = tk - (m - 1)
                                S.op('act', lambda e, gb=gb, m=m, tk0=tk0: e.copy(Sst[:, sb_, :, tk0:tk0 + m], gp[:, gb, 0:m, :].rearrange("p t i -> p i t")),
                                     reads=['gp%d' % gb], writes=['Sst%d' % sb_])
                            if tk % 32 == 31 and pending:
                                qproj_cb(ti + 1, pending.pop(0))
                        segs = []
                        a0 = r0
                        while a0 < r0 + n:
                            tl = a0 // TILE
                            a1 = min(r0 + n, (tl + 1) * TILE)
                            segs.append((tl, a0 - tl * TILE, a0 - r0, a1 - a0))
                            a0 = a1
                        for (tl, off, s0, ln) in segs:
                            for i1a in range(0, 128, 16):
                                S.dma('sp', GT[tl].rearrange("a b n -> b a n")[:, i1a:i1a + 16, off:off + ln], Sst[:, sb_, i1a:i1a + 16, s0:s0 + ln],
                                      reads=['Sst%d' % sb_], writes=[('GT', tl)], sem='st_Sst%d' % sb_)
                    while pending:
                        qproj_cb(ti + 1, pending.pop(0))
                S.barrier()
            chk('R%d' % l)
            with ExitStack() as es:
                acc = T(es, "acc", [128, 9, 2048], F32)
                hp = P(es, "hp", [128, 3, 512])
                op_ = P(es, "op", [128, 2, 1024])
                xa_src = xa_d
                for tl in range(2):
                    esd = es.enter_context(ExitStack())
                    stg = T(esd, "stgD", [128, 2, 2048], F32)
                    uTb = T(esd, "uTb", [128, 2, 16, 128], BF16)
                    vb = T(esd, "vb", [128, 2, 2, 2048], BF16)
                    gt = T(esd, "gt", [128, 2, TILE], BF16)
                    hg = T(esd, "hg", [128, 2, TILE], BF16)
                    wT = T(esd, "wT", [128, 2, 2, TILE], BF16)
                    t0 = tl * TILE
                    LB = [(i * 128, 128) for i in range(8)] + [(1024, 16)]
                    NTL = [(0, 512), (512, 512), (1024, 16)]
                    for bi, (o, n) in enumerate(LB):
                        S.dma('sp', acc[:n, bi, :], xa_src[t0 + o:t0 + o + n, :], reads=[('xa_d', 0)], writes=[('acc', bi)])
                        S.op('pool', lambda e, bi=bi, n=n: e.tensor_scalar(acc[:n, bi, :], acc[:n, bi, :], ALPHA, 0.0, op0=ALU.mult, op1=ALU.add),
                             reads=[('acc', bi)], writes=[('acc', bi)])
                    info = {}
                    pre_U = set()

                    def loads(i1):
                        sU = S.rot("stgD", 2)
                        ub = S.rot('uTb', 2)
                        S.dma('sp', stg[:, sU, :], p_uT[l, i1].rearrange("p c e -> p (c e)"), writes=['stgD%d' % sU])
                        S.op('pool', lambda e: e.tensor_copy(uTb[:, ub, :, :].rearrange("p c e -> p (c e)"), stg[:, sU, :]),
                             reads=['stgD%d' % sU], writes=['uTb%d' % ub])
                        sV = S.rot("stgD", 2)
                        grp = (i1 // 2) % 2
                        slot = i1 % 2
                        S.dma('sp', stg[:, sV, :], p_v[l, i1 * 128:(i1 + 1) * 128, :], writes=['stgD%d' % sV])
                        S.op('act', lambda e: e.copy(vb[:, grp, slot, :], stg[:, sV, :]),
                             reads=['stgD%d' % sV], writes=[('vb', grp, slot)])
                        gb = S.rot('gt', 2)
                        S.dma('sp', gt[:, gb, :], GT[tl, i1], reads=[('GT', tl)], writes=['gt%d' % gb])
                        info[i1] = (ub, grp, slot, gb)

                    def u_closures(i1):
                        ub = info[i1][0]
                        cl = []
                        for ni, (c0, w) in enumerate(NTL):
                            for dc in range(16):
                                cl.append(lambda ni=ni, c0=c0, w=w, dc=dc: S.op('pe', lambda e: e.matmul(
                                    hp[:, ni, :w], lhsT=uTb[:, ub, dc, :], rhs=AT[:, dc, t0 + c0:t0 + c0 + w], start=(dc == 0), stop=(dc == 15)),
                                    reads=['uTb%d' % ub] + ak(t0 + c0, w), writes=['hp%d' % ni]))
                        return cl

                    def post(i1):
                        (ub, grp, slot, gb) = info[i1]
                        hb_ = S.rot('hg', 2)
                        for ni, (c0, w) in enumerate(NTL):
                            S.op('act', lambda e, ni=ni, c0=c0, w=w: e.activation(hg[:, hb_, c0:c0 + w], hp[:, ni, :w], AF.Gelu_apprx_tanh),
                                 reads=['hp%d' % ni], writes=[('hg', hb_, ni)])
                        S.op('dve', lambda e: e.tensor_tensor(wT[:, grp, slot, :], hg[:, hb_, :], gt[:, gb, :], op=ALU.mult),
                             reads=[('hg', hb_, 0), ('hg', hb_, 1), ('hg', hb_, 2), 'gt%d' % gb], writes=[('wT', grp, slot)])

                    for i1 in range(ni1):
                        if i1 not in info:
                            loads(i1)
                        if i1 not in pre_U:
                            for c in u_closures(i1):
                                c()
                        post(i1)
                        (ub, grp, slot, gb) = info[i1]
                        if slot == 1:
                            nxt = []
                            if i1 + 1 < ni1:
                                loads(i1 + 1)
                                nxt = u_closures(i1 + 1)
                                pre_U.add(i1 + 1)
                            for bi, (o, n) in enumerate(LB):
                                for half in range(2):
                                    pb = S.rot('op', 2)
                                    for cs in range(2):
                                        for s_ in range(2):
                                            cc = (half * 2 + cs) * 512
                                            S.op('pe', lambda e, cs=cs, s_=s_, cc=cc: e.matmul(
                                                op_[:n, pb, cs * 512:(cs + 1) * 512], lhsT=wT[:, grp, s_, o:o + n], rhs=vb[:, grp, s_, cc:cc + 512],
                                                start=(s_ == 0), stop=(s_ == 1)),
                                                reads=[('wT', grp, s_), ('vb', grp, s_)], writes=['op%d_%d' % (pb, cs)])
                                    S.op('dve', lambda e, half=half: e.tensor_tensor(
                                        acc[:n, bi, half * 1024:(half + 1) * 1024], acc[:n, bi, half * 1024:(half + 1) * 1024], op_[:n, pb, :], op=ALU.add),
                                        reads=['op%d_0' % pb, 'op%d_1' % pb, ('acc', bi)], writes=[('acc', bi)])
                                    for _ in range(3):
                                        if nxt:
                                            nxt.pop(0)()
                            while nxt:
                                nxt.pop(0)()
                    S.barrier()
                    esd.close()
                    with ExitStack() as es2:
                        xo = T(es2, "xoD", [128, 2048], F32)
                        gam = T(es2, "gamD", [128, 2048], F32)
                        bet = T(es2, "betD", [128, 2048], F32)
                        S.dma('sp', gam[:], ln_g[lni], writes=['lnp'])
                        S.dma('sp', bet[:], ln_b[lni], writes=['lnp'])
                        tpv = op_[:, :, :].rearrange("p a (c t) -> p (a c) t", t=128)
                        opk = ['op0_0', 'op0_1', 'op1_0', 'op1_1']
                        for bi, (o, n) in enumerate(LB):
                            r0 = t0 + o
                            layer_norm(acc[:, bi, :], n, gam, bet, xo, ('acc', bi), 'xo')
                            if final_out:
                                S.dma('sp', o_y[r0:r0 + n, :], xo[:n, :], reads=['xo'], writes=['o_y'], sem='st_xoD')
                            else:
                                S.dma('sp', dst_d[r0:r0 + n, :], xo[:n, :], reads=['xo'], writes=[('x1_d', 0)], sem='st_xoD')
                                for ch in range(16):
                                    S.op('pe', lambda e, ch=ch: e.transpose(tpv[:, ch, :n], xo[:n, ch * 128:(ch + 1) * 128], ident[:n, :n]),
                                         reads=['xo', 'ident'], writes=opk)
                                S.op('act', lambda e: e.copy(AT[:, 0:8, r0:r0 + n], tpv[:, 0:8, :n]), reads=opk, writes=ak(r0, n) + ['_x'])
                                S.op('dve', lambda e: e.tensor_copy(AT[:, 8:16, r0:r0 + n], tpv[:, 8:16, :n]), reads=opk, writes=ak(r0, n) + ['_y'])
                    S.barrier()

        wo_ln(w_o0, x_all, 0, xa_d)
        chk('E0')
        peer(0, 1, x1_d, False)
        chk('P0')

        with ExitStack() as es:
            oth = T(es, "oth", [128, 2, NT], BF16)
            stg = T(es, "stgE", [128, 16, 256], F32)
            wh = T(es, "wh", [128, 3, 16, 256], BF16)
            qkT = T(es, "qkT", [128, 2, NT], BF16)
            kkT = T(es, "kkT", [128, 2, NKC], BF16)
            Vd = T(es, "Vd", [128, 25, 256], BF16)
            kvo = T(es, "kvo", [128, 2, 256], F32)
            cb_t = T(es, "cb_t", [128, 2, 256], F32)
            EB_sb = T(es, "EBd_sb", [128, 8, 128], F32)
            btab = T(es, "btab", [128, 8, 17], F32)
            lam_t = T(es, "lam_t", [128, 4, 128], F32)
            lam_s = T(es, "lam_s", [128, 4], F32)
            gs_t = T(es, "gs_t", [128, 2], F32)
            pt = T(es, "ptd", [128, 2, 512], BF16)
            ptf = T(es, "ptfd", [128, 128], F32)
            rc = T(es, "rcd", [128, 512], F32)
            On = T(es, "Ond", [128, 2, 512], F32)
            O1 = T(es, "O1", [128, 2, NT], F32)
            sq = T(es, "sq", [128, 2, 512], BF16)
            st_ps = P(es, "st_psd", [128, 2, 512])
            oacc = P(es, "oaccd", [128, 2, 512])
            sacc = P(es, "saccd", [128, 512])
            pq = P(es, "pqd", [128, 512])
            tp = P(es, "tpd", [128, 2, 128])
            biasTab_sb = btab
            S.dma('sp', EB_sb[:], EBd[:, :, :], writes=['EB'])
            S.dma('sp', btab[:], biasTab[:, :, :], writes=['btab'])
            S.dma('sp', lam_t[:], lamv[:, :, :], writes=['lam_t'])
            S.dma('sp', gs_t[:], gsub[:, :], writes=['gs_t'])
            S.op('dve', lambda e: e.tensor_tensor(lam_t[:, 0, :], lam_t[:, 0, :], lam_t[:, 1, :], op=ALU.mult), reads=['lam_t'], writes=['lam_t'])
            S.op('dve', lambda e: e.tensor_tensor(lam_t[:, 2, :], lam_t[:, 2, :], lam_t[:, 3, :], op=ALU.mult), reads=['lam_t'], writes=['lam_t'])
            S.op('dve', lambda e: e.tensor_reduce(lam_s[:, 0:1], lam_t[:, 0, :], axis=AX.X, op=ALU.add), reads=['lam_t'], writes=['lam_s'])
            S.op('dve', lambda e: e.tensor_reduce(lam_s[:, 1:2], lam_t[:, 2, :], axis=AX.X, op=ALU.add), reads=['lam_t'], writes=['lam_s'])
            S.op('act', lambda e: e.activation(lam_s[:, 0:2], lam_s[:, 0:2], AF.Exp), reads=['lam_s'], writes=['lam_s'])
            S.op('dve', lambda e: e.tensor_tensor(lam_s[:, 2:3], lam_s[:, 0:1], lam_s[:, 1:2], op=ALU.subtract), reads=['lam_s'], writes=['lam_s'])
            S.op('dve', lambda e: e.tensor_scalar(lam_s[:, 3:4], lam_s[:, 2:3], LAM_INIT, -1.0, op0=ALU.add, op1=ALU.mult), reads=['lam_s'], writes=['lam_s'])
            S.op('dve', lambda e: e.tensor_scalar(gs_t[:, :], gs_t[:, :], 1.0 - LAM_INIT, None, op0=ALU.mult), reads=['gs_t'], writes=['gs_t'])
            wv = w_qkv.rearrange("(c p) n -> p c n", p=128)
            KT = [(i * 512, 512) for i in range(4)] + [(2048, 32)]
            VB = [(i * 128, 128) for i in range(16)] + [(2048, 32)] + [(2080 + 128 * j, 128) for j in range(8)]
            for h in range(nheads1):
                for j, c0 in enumerate((h * 256, 2048 + h * 256, 4096 + h * 256)):
                    S.dma('sp', stg[:, :, :], wv[:, :, c0:c0 + 256], writes=['stgE'])
                    S.op(('act', 'dve', 'pool')[j], (lambda e, j=j: e.copy(wh[:, j, :, :], stg[:, :, :])) if j == 0 else (lambda e, j=j: e.tensor_copy(wh[:, j, :, :], stg[:, :, :])), reads=['stgE'], writes=['wh%d' % j])
                chk('D1')
                import os as _os
                _dj = _os.environ.get('DBG_J')
                for j, dstT in ((0, qkT), (1, kkT)):
                    if _dj is not None and str(j) not in _dj:
                        continue
                    for s2 in range(2):
                        for (t0, n) in KT:
                            for dc in range(16):
                                S.op('pe', lambda e, j=j, s2=s2, dc=dc: e.matmul(pq[:, :n], lhsT=wh[:, j, dc, s2 * 128:(s2 + 1) * 128], rhs=AT[:, dc, t0:t0 + n],
                                                                                 start=(dc == 0), stop=(dc == 15)),
                                     reads=['wh%d' % j] + ak(t0, n), writes=['pqd'])
                            S.op('act', lambda e, dstT=dstT, s2=s2: e.copy(dstT[:, s2, t0:t0 + n], pq[:, :n]), reads=['pqd'], writes=['qkT' if j == 0 else 'kkT'])
                chk('D2')
                for bi, (r0, n) in enumerate(TB):
                    for j, (od, kk) in ((1, (o_dk, 0)), (2, (o_dv, 1))):
                        for dc in range(16):
                            S.op('pe', lambda e, j=j, dc=dc: e.matmul(pq[:n, 0:256], lhsT=AT[:, dc, r0:r0 + n], rhs=wh[:, j, dc, :], start=(dc == 0), stop=(dc == 15)),
                                 reads=['wh%d' % j] + ak(r0, n), writes=['pqd'])
                        S.op('act', lambda e, kk=kk: e.copy(kvo[:n, kk, :], pq[:n, 0:256]), reads=['pqd'], writes=['kvo%d' % kk])
                        if kk == 1:
                            S.op('dve', lambda e: e.tensor_copy(Vd[:n, bi, :], kvo[:n, 1, :]), reads=['kvo1'], writes=['Vd'])
                        if _os.environ.get('DBG_NOKV') is None:
                            S.dma('sp', od[r0:r0 + n, h * 256:(h + 1) * 256], kvo[:n, kk, :], reads=['kvo%d' % kk], writes=['o_dkv%d' % kk], sem='st_kvo%d' % kk)
                chk('D3')
                for jb in range(8):
                    S.dma('sp', cb_t[:, 0, :], c_dk[jb * 128:(jb + 1) * 128, h * 256:(h + 1) * 256], writes=['cbt0'])
                    S.dma('sp', cb_t[:, 1, :], c_dv[jb * 128:(jb + 1) * 128, h * 256:(h + 1) * 256], writes=['cbt1'])
                    for s2 in range(2):
                        S.op('pe', lambda e, s2=s2: e.transpose(tp[:, s2, :], cb_t[:, 0, s2 * 128:(s2 + 1) * 128], ident[:, :]), reads=['cbt0', 'ident'], writes=['tpd'])
                    S.op('act', lambda e: e.copy(kkT[:, :, 2080 + jb * 128:2080 + (jb + 1) * 128], tp[:, :, :]), reads=['tpd'], writes=['kkT'])
                    S.op('dve', lambda e: e.tensor_copy(Vd[:, 17 + jb, :], cb_t[:, 1, :]), reads=['cbt1'], writes=['Vd'])
                chk('D4')
                for s2 in range(2):
                    def cb(qc0, nq, s2=s2, h=h):
                        if s2 == 0:
                            S.op('pool', lambda e: e.tensor_copy(O1[:, :, qc0:qc0 + nq], On[:, :, :nq]), reads=[('On', 0), ('On', 1)], writes=['O1'])
                        else:
                            for vc in range(2):
                                S.op('dve', lambda e, vc=vc: e.scalar_tensor_tensor(On[:, vc, :nq], On[:, vc, :nq], lam_s[:, 3:4], O1[:, vc, qc0:qc0 + nq], op0=ALU.mult, op1=ALU.add),
                                     reads=[('On', vc), 'O1', 'lam_s'], writes=[('On', vc)])
                                S.op('act', lambda e, vc=vc: e.activation(sq[:, vc, :nq], On[:, vc, :nq], AF.Square), reads=[('On', vc)], writes=[('sq', vc)])
                            for vc in range(2):
                                S.op('pe', lambda e, vc=vc: e.matmul(sacc[:, :nq], lhsT=ones_b[:, :], rhs=sq[:, vc, :nq], start=(vc == 0), stop=(vc == 1)),
                                     reads=[('sq', vc), 'ones_b'], writes=['sacc'])
                            S.op('dve', lambda e: e.tensor_scalar(rc[:, :nq], sacc[:, :nq], 1.0 / 256, RMS_EPS, op0=ALU.mult, op1=ALU.add), reads=['sacc'], writes=['rc'])
                            S.op('act', lambda e: e.activation(rc[:, :nq], rc[:, :nq], AF.Sqrt), reads=['rc'], writes=['rc'])
                            S.op('dve', lambda e: e.reciprocal(rc[:, :nq], rc[:, :nq]), reads=['rc'], writes=['rc'])
                            for vc in range(2):
                                S.op('dve', lambda e, vc=vc: e.scalar_tensor_tensor(oth[:, vc, qc0:qc0 + nq], On[:, vc, :nq], gs_t[:, vc:vc + 1], rc[:, :nq],
                                                                                    op0=ALU.mult, op1=ALU.mult),
                                     reads=[('On', vc), 'rc', 'gs_t'], writes=['oth'])
                    pieces = [(kkT[:, s2, :], ['kkT'], qkT[:, s2, :], ['qkT'], 128)]
                    attention(es, pieces, Vd, 2, prompt_groups() + sample_groups(), 128 ** -0.5, h, EB_sb[:, h, :],
                              (st_ps, oacc, sacc), (pt, ptf, rc, On), cb, 'd')
                chk('D5')
                for vc in range(2):
                    S.dma('sp', ot_d[2 * h + vc], oth[:, vc, :], reads=['oth'], writes=['ot_d'], sem='st_oth')
            S.barrier()
            for c in range(16):
                S.dma('sp', AT[:, c, :], ot_d[c], reads=['ot_d'], writes=[('ATall', c)], sem='ds_ATall')
            S.barrier()

        chk('DF')
        wo_ln(w_o1, x1_d, 2, xa_d)
        chk('E1')
        peer(1, 3, None, True)
        S.barrier()
    except _Stop:
        pass
    return nc


_NC = None


def _host_consts():
    c = {}
    c["c_ident"] = np.eye(128, dtype=np.float32)
    c["c_iota"] = np.tile(np.arange(128, dtype=np.float32)[None, :], (128, 1))
    c["c_iota16"] = np.tile(np.arange(16, dtype=np.float32)[None, :], (128, 1))
    pos = np.concatenate([np.arange(2048), 1024 + np.arange(32)]).astype(np.float32)
    half = 32
    inv = (10000.0 ** (-np.arange(half, dtype=np.float32) / half)).astype(np.float32)
    ang = (pos[:, None] * inv[None, :]).astype(np.float32)
    cos = np.cos(ang).astype(np.float32)
    sin = np.sin(ang).astype(np.float32)
    c["cosT"] = cos
    c["sinT"] = sin
    c["cosF"] = np.ascontiguousarray(np.concatenate([cos, cos], 1).T)
    c["sinF"] = np.ascontiguousarray(np.concatenate([-sin, sin], 1).T)
    ki = np.arange(128)[:, None]
    qi = np.arange(128)[None, :]
    vis = ((ki // 64) <= (qi // 64)).astype(np.float64)
    c["EBm"] = vis.astype(np.float32)
    slopes = 2.0 ** (-8.0 * np.arange(1, 9) / 8)
    EBd = np.zeros((128, 8, 128), np.float64)
    for h in range(8):
        EBd[:, h, :] = np.exp(-slopes[h] * np.abs(qi - ki) + slopes[h] * qi) * vis
    c["EBd"] = EBd.astype(np.float32)
    bt = np.zeros((128, 8, 17), np.float64)
    for h in range(8):
        for d in range(17):
            bt[:, h, d] = -slopes[h] * 128.0 * d + slopes[h] * np.arange(128)
    c["biasTab"] = bt.astype(np.float32)
    return c


def kernel(x_prompt, x_sample, cache_mla_ckv, cache_mla_krope, cache_diff_k, cache_diff_v,
           mla_w_dqkv, mla_g_q, mla_w_uq, mla_g_kv, mla_w_ukv, mla_w_o,
           diff_w_qkv, diff_lam_q1, diff_lam_k1, diff_lam_q2, diff_lam_k2, diff_g_sub, diff_w_o,
           peer_w_query, peer_sub_keys, peer_u, peer_v,
           ln_mix_g, ln_mix_b, ln_ffn_g, ln_ffn_b):
    global _NC
    f = lambda a: np.ascontiguousarray(np.asarray(a, dtype=np.float32))
    x_prompt, x_sample = f(x_prompt), f(x_sample)
    if _NC is None:
        _NC = build_nc()
    nc = _NC
    consts = _host_consts()
    w_uq = f(mla_w_uq)[0]
    wr = w_uq.reshape(512, 16, 192)[:, :, 128:192]
    w_uqs = np.ascontiguousarray(np.concatenate([wr[:, :, 32:64], wr[:, :, 0:32]], -1).reshape(512, 1024))
    p_skT = np.ascontiguousarray(np.transpose(f(peer_sub_keys).reshape(2, 16, 128, 128), (0, 3, 1, 2)))
    pu = f(peer_u).reshape(2, 128, 128, 16, 128)
    p_uT = np.ascontiguousarray(np.transpose(pu, (0, 1, 4, 3, 2)))
    bc = lambda v, n: np.ascontiguousarray(np.broadcast_to(f(v).reshape(1, n), (128, n)))
    lng = np.stack([bc(ln_mix_g[0], 2048), bc(ln_ffn_g[0], 2048), bc(ln_mix_g[1], 2048), bc(ln_ffn_g[1], 2048)])
    lnb = np.stack([bc(ln_mix_b[0], 2048), bc(ln_ffn_b[0], 2048), bc(ln_mix_b[1], 2048), bc(ln_ffn_b[1], 2048)])
    gsub = np.ascontiguousarray(f(diff_g_sub)[0].reshape(2, 128).T)
    lamv = np.ascontiguousarray(np.stack([bc(diff_lam_q1[0], 128), bc(diff_lam_k1[0], 128), bc(diff_lam_q2[0], 128), bc(diff_lam_k2[0], 128)], 1))
    shared = dict(
        w_dqkv=f(mla_w_dqkv)[0], w_uq=w_uq, w_uqs=w_uqs, w_ukv=f(mla_w_ukv)[0], w_o0=f(mla_w_o)[0],
        w_qkv=f(diff_w_qkv)[0], w_o1=f(diff_w_o)[0], p_wq=np.ascontiguousarray(np.transpose(f(peer_w_query).reshape(2, 16, 128, 16, 128), (0, 3, 2, 1, 4))), p_skT=p_skT, p_uT=p_uT, p_v=f(peer_v),
        gq_bc=bc(mla_g_q[0], 512), gkv_bc=bc(mla_g_kv[0], 512), ln_g=lng, ln_b=lnb, gsub=gsub, lamv=lamv, **consts)
    in_maps = []
    for c in range(8):
        m = dict(shared)
        m["x_all"] = np.ascontiguousarray(np.concatenate([x_prompt[c // 2], x_sample[c]], 0))
        m["c_ckv"] = f(cache_mla_ckv)[0, c]
        m["c_kr"] = f(cache_mla_krope)[0, c]
        m["c_dk"] = np.ascontiguousarray(f(cache_diff_k)[0, c].reshape(1024, 2048))
        m["c_dv"] = np.ascontiguousarray(f(cache_diff_v)[0, c].reshape(1024, 2048))
        in_maps.append(m)
    if _NI1[0] < 128:
        shared['p_uT'] = np.ascontiguousarray(shared['p_uT'][:, :max(_NI1[0], 1)])
        shared['p_v'] = np.ascontiguousarray(shared['p_v'][:, :max(_NI1[0], 1) * 128])
        for m in in_maps:
            m['p_uT'] = shared['p_uT']
            m['p_v'] = shared['p_v']
    in_maps = [{k: v for k, v in m.items() if k in _DECL} for m in in_maps]
    res = run_bass_kernel_spmd(nc, in_maps, core_ids=list(range(8)))
    R = res.results
    pr = lambda k, w: np.stack([R[2 * b][k][:2048] for b in range(4)])
    sm = lambda k: np.stack([R[c][k][2048:2080] for c in range(8)])
    y_p = pr("o_y", 2048)
    y_s = sm("o_y")
    return (y_p, y_s,
            pr("o_ckv", 512)[None], pr("o_kr", 64)[None],
            pr("o_dk", 2048).reshape(1, 4, 2048, 8, 256), pr("o_dv", 2048).reshape(1, 4, 2048, 8, 256),
            sm("o_ckv")[None], sm("o_kr")[None],
            sm("o_dk").reshape(1, 8, 32, 8, 256), sm("o_dv").reshape(1, 8, 32, 8, 256))
```

```python
import math
import numpy as np
from contextlib import ExitStack
import concourse.bass as bass
import concourse.mybir as mybir
from concourse.bass_utils import run_bass_kernel_spmd

F32 = mybir.dt.float32
BF16 = mybir.dt.bfloat16
U32 = mybir.dt.uint32
AF = mybir.ActivationFunctionType
ALU = mybir.AluOpType
AX = mybir.AxisListType

NT, NPR, NSM, DM = 2080, 2048, 32, 2048
NKC = 3104
ALPHA = 4 ** 0.25
BETA = 16 ** -0.25
LN_EPS = 1e-5
RMS_EPS = 1e-6
TB = [(i * 128, 128) for i in range(16)] + [(2048, 32)]
TT = [(i * 512, 512) for i in range(4)] + [(2048, 32)]
TILE = 1040
LAM_INIT = 0.8 - 0.6 * math.exp(-0.3 * 1)


class Sync:
    def __init__(self, nc, es):
        self.nc = nc
        self.es = es
        self.eng = {'pe': nc.tensor, 'act': nc.scalar, 'dve': nc.vector, 'pool': nc.gpsimd, 'sp': nc.sync}
        self.semobj = {}
        self.ecnt = {}
        for k in self.eng:
            self.semobj['es_' + k] = es.enter_context(nc.semaphore('es_' + k))
            self.ecnt[k] = 0
        self.waited = {k: {} for k in self.eng}
        self.keys = {}
        self.dcnt = {}
        self.rotc = {}
        self.alias = {}
        self.free = []

    def rot(self, name, n):
        c = self.rotc.get(name, 0)
        self.rotc[name] = c + 1
        return c % n

    def _key(self, k):
        if k not in self.keys:
            self.keys[k] = {'w': None, 'r': {}}
        return self.keys[k]

    def _waits(self, en, reads, writes):
        need = {}

        def add(ev):
            if ev is None:
                return
            s, v = ev
            if need.get(s, 0) < v:
                need[s] = v
        for k in reads:
            add(self._key(k)['w'])
        for k in writes:
            kk = self._key(k)
            add(kk['w'])
            for ev in kk['r'].values():
                add(ev)
        e = self.eng[en]
        for sname, v in need.items():
            if en == 'pe' and sname == 'es_pe':
                continue
            if self.waited[en].get(sname, 0) >= v:
                continue
            e.wait_ge(self.semobj[sname], v)
            self.waited[en][sname] = v

    def _record(self, ev, reads, writes):
        for k in writes:
            kk = self._key(k)
            kk['w'] = ev
            kk['r'] = {}
        for k in reads:
            if k in writes:
                continue
            self._key(k)['r'][ev[0]] = ev

    def op(self, en, fn, reads=(), writes=()):
        self._waits(en, reads, writes)
        ins = fn(self.eng[en])
        self.ecnt[en] += 1
        sname = 'es_' + en
        ins.then_inc(self.semobj[sname], 1)
        self._record((sname, self.ecnt[en]), reads, writes)

    def dma(self, q, out, in_, reads=(), writes=(), sem=None):
        if sem is None:
            sem = 'ds_' + str(writes[0])
        can = self.alias.get(sem)
        if can is None:
            if self.free:
                can = self.free.pop()
            else:
                can = 'dq%d' % len(self.dcnt)
                self.semobj[can] = self.es.enter_context(self.nc.semaphore(can))
                self.dcnt[can] = 0
            self.alias[sem] = can
        self._waits(q, reads, writes)
        ins = self.eng[q].dma_start(out=out, in_=in_)
        self.dcnt[can] += 16
        ins.then_inc(self.semobj[can], 16)
        self._record((can, self.dcnt[can]), reads, writes)

    def barrier(self):
        for en, e in self.eng.items():
            for sname, c in self.dcnt.items():
                if c > 0 and self.waited[en].get(sname, 0) < c:
                    e.wait_ge(self.semobj[sname], c)
                    self.waited[en][sname] = c
            for k, c in self.ecnt.items():
                sname = 'es_' + k
                if c > 0 and k != en and self.waited[en].get(sname, 0) < c:
                    e.wait_ge(self.semobj[sname], c)
                    self.waited[en][sname] = c
        self.free = sorted(self.dcnt.keys(), reverse=True)
        self.alias = {}


_DECL = []
_NI1 = [128]


class _Stop(Exception):
    pass


def build_nc(stop=None, nheads0=16, nheads1=8, ni1=128):
    nc = bass.Bass("TRN2", target_bir_lowering=False)

    big = stop in (None, 'P0', 'DF', 'E1', 'R1') and ni1 > 0
    _DECL.clear()

    def din(name, shape, dt=F32):
        if name in ('p_uT', 'p_v') and not big:
            return None
        _DECL.append(name)
        return nc.dram_tensor(name, list(shape), dt, kind="ExternalInput").ap()

    def dout(name, shape):
        return nc.dram_tensor(name, list(shape), F32, kind="ExternalOutput").ap()

    x_all = din("x_all", [NT, DM])
    c_ckv = din("c_ckv", [1024, 512])
    c_kr = din("c_kr", [1024, 64])
    c_dk = din("c_dk", [1024, 2048])
    c_dv = din("c_dv", [1024, 2048])
    w_dqkv = din("w_dqkv", [2048, 1088])
    w_uq = din("w_uq", [512, 3072])
    w_uqs = din("w_uqs", [512, 1024])
    w_ukv = din("w_ukv", [512, 4096])
    w_o0 = din("w_o0", [2048, 2048])
    w_qkv = din("w_qkv", [2048, 6144])
    w_o1 = din("w_o1", [2048, 2048])
    p_wq = din("p_wq", [2, 16, 128, 16, 128])
    p_skT = din("p_skT", [2, 128, 16, 128])
    p_uT = din("p_uT", [2, max(ni1, 1), 128, 16, 128])
    p_v = din("p_v", [2, max(ni1, 1) * 128, 2048])
    _NI1[0] = ni1
    gq_bc = din("gq_bc", [128, 512])
    gkv_bc = din("gkv_bc", [128, 512])
    ln_g = din("ln_g", [4, 128, 2048])
    ln_b = din("ln_b", [4, 128, 2048])
    gsub = din("gsub", [128, 2])
    lamv = din("lamv", [128, 4, 128])
    c_ident = din("c_ident", [128, 128])
    c_iota = din("c_iota", [128, 128])
    c_iota16 = din("c_iota16", [128, 16])
    cosT = din("cosT", [NT, 32])
    sinT = din("sinT", [NT, 32])
    cosF = din("cosF", [64, NT])
    sinF = din("sinF", [64, NT])
    EBm = din("EBm", [128, 128])
    EBd = din("EBd", [128, 8, 128])
    biasTab = din("biasTab", [128, 8, 17])

    o_y = dout("o_y", [NT, DM])
    o_ckv = dout("o_ckv", [NT, 512])
    o_kr = dout("o_kr", [NT, 64])
    o_dk = dout("o_dk", [NT, 2048])
    o_dv = dout("o_dv", [NT, 2048])

    xa_d = nc.dram_tensor("xa_d", [NT, DM], F32, kind="Internal").ap()
    x1_d = nc.dram_tensor("x1_d", [NT, DM], F32, kind="Internal").ap()
    GT = nc.dram_tensor("GT", [2, 128, 128, TILE], BF16, kind="Internal").ap()
    ot_d = nc.dram_tensor("ot_d", [16, 128, NT], BF16, kind="Internal").ap()

    try:
      with ExitStack() as es0:
        S = Sync(nc, es0)

        def chk(name):
            if stop == name:
                S.barrier()
                src = {'E0': xa_d, 'P0': x1_d, 'E1': xa_d}.get(name)
                if src is not None:
                    for r0 in range(0, NT, 260):
                        S.dma('sp', o_y[r0:r0 + 260, :], src[r0:r0 + 260, :], writes=['dbg_oy'], sem='dbg_oy')
                    S.barrier()
                raise _Stop()

        uid = [0]

        def T(es, name, shape, dt):
            uid[0] += 1
            return es.enter_context(nc.sbuf_tensor("%s_%d" % (name, uid[0]), list(shape), dt))

        def P(es, name, shape, dt=F32):
            uid[0] += 1
            return es.enter_context(nc.psum_tensor("%s_%d" % (name, uid[0]), list(shape), dt))

        AT = T(es0, "AT", [128, 16, NT], BF16)
        ident = T(es0, "ident", [128, 128], F32)
        iotaF = T(es0, "iotaF", [128, 128], F32)
        iota16 = T(es0, "iota16", [128, 16], F32)
        ones_f = T(es0, "ones_f", [128, 128], F32)
        ones_b = T(es0, "ones_b", [128, 128], BF16)
        S.dma('sp', ident[:], c_ident[:, :], writes=['ident'])
        S.dma('sp', iotaF[:], c_iota[:, :], writes=['iotaF'])
        S.dma('sp', iota16[:], c_iota16[:, :], writes=['iota16'])
        S.op('dve', lambda e: e.memset(ones_f[:], 1.0), writes=['ones_f'])
        S.op('dve', lambda e: e.tensor_copy(ones_b[:], ones_f[:]), reads=['ones_f'], writes=['ones_b'])

        def atk(c0, n):
            ks = set()
            c = c0
            while c < c0 + n:
                ks.add(('AT', c // 16))
                c += 16
            return sorted(ks)

        def atk_blocks(c0, n):
            return [('ATB', b) for b in range(c0 // 16, (c0 + n + 15) // 16)]

        def ak(c0, n):
            return [('AT', g) for g in range(c0 // 16, (c0 + n - 1) // 16 + 1)]

        def transpose_to(es_tp, tp, src, n, nch, dst_ap_fn, rkeys, wkeys, np_=128):
            for ch in range(nch):
                S.op('pe', lambda e, ch=ch: e.transpose(tp[:np_, ch, :n], src[:n, ch * np_:(ch + 1) * np_], ident[:n, :n]),
                     reads=list(rkeys) + ['ident'], writes=['tp'])
            if nch >= 8:
                h = nch // 2
                S.op('act', lambda e: e.copy(dst_ap_fn(0, h), tp[:np_, 0:h, :n]), reads=['tp'], writes=list(wkeys) + ['_tpa'])
                S.op('dve', lambda e: e.tensor_copy(dst_ap_fn(h, nch), tp[:np_, h:nch, :n]), reads=['tp'], writes=list(wkeys) + ['_tpb'])
            else:
                S.op('act', lambda e: e.copy(dst_ap_fn(0, nch), tp[:np_, 0:nch, :n]), reads=['tp'], writes=list(wkeys))

        def layer_norm(y, n, gam, bet, out, ykey, okey):
            st = lnst
            for j in range(4):
                S.op('dve', lambda e, j=j: e.bn_stats(st[:n, j, :], y[:n, j * 512:(j + 1) * 512]), reads=[ykey], writes=['lnst%d' % j])
            S.op('dve', lambda e: e.bn_aggr(mv[:n, :], st[:n, :, :].rearrange("p a b -> p (a b)")),
                 reads=['lnst0', 'lnst1', 'lnst2', 'lnst3'], writes=['mv'])
            S.op('dve', lambda e: e.tensor_scalar(rs[:n, 0:1], mv[:n, 1:2], LN_EPS, None, op0=ALU.add), reads=['mv'], writes=['rs'])
            S.op('act', lambda e: e.activation(rs[:n, 0:1], rs[:n, 0:1], AF.Sqrt), reads=['rs'], writes=['rs'])
            S.op('dve', lambda e: e.reciprocal(rs[:n, 0:1], rs[:n, 0:1]), reads=['rs'], writes=['rs'])
            S.op('dve', lambda e: e.scalar_tensor_tensor(rs[:n, 1:2], mv[:n, 0:1], -1.0, rs[:n, 0:1], op0=ALU.mult, op1=ALU.mult),
                 reads=['rs', 'mv'], writes=['rs2'])
            S.op('act', lambda e: e.activation(out[:n, :], y[:n, :], AF.Identity, bias=rs[:n, 1:2], scale=rs[:n, 0:1]),
                 reads=[ykey, 'rs', 'rs2'], writes=[okey])
            S.op('pool', lambda e: e.tensor_tensor(out[:n, :], out[:n, :], gam[:n, :], op=ALU.mult), reads=[okey, 'lnp'], writes=[okey])
            S.op('pool', lambda e: e.tensor_tensor(out[:n, :], out[:n, :], bet[:n, :], op=ALU.add), reads=[okey, 'lnp'], writes=[okey])

        lnst = T(es0, "lnst", [128, 4, 6], F32)
        mv = T(es0, "mv", [128, 2], F32)
        rs = T(es0, "rs", [128, 4], F32)
        ss = T(es0, "ss", [128, 2], F32)

        def attention(es_a, pieces, V, nvc, groups, scale, hb, EB, psum, work, out_cb, tag):
            st_ps, oacc, sacc = psum
            pt, ptf, rc, On = work
            for (qc0, qblocks, kblocks) in groups:
                nq = sum(w for _, w in qblocks)
                nkb = len(kblocks)
                bsel = {}

                def emit_scores(idx):
                    (kc0, nk, vblk, i0, deltas) = kblocks[idx]
                    b = S.rot('st' + tag, 2)
                    bsel[idx] = b
                    o0 = qblocks[i0][0]
                    ncol = nq - o0
                    for pi, (kT, kkeys, qT, qkeys, Kp) in enumerate(pieces):
                        S.op('pe', lambda e, kT=kT, qT=qT, Kp=Kp, pi=pi: e.matmul(
                            st_ps[:nk, b, o0:o0 + ncol], lhsT=kT[:Kp, kc0:kc0 + nk], rhs=qT[:Kp, qc0 + o0:qc0 + o0 + ncol],
                            start=(pi == 0), stop=(pi == len(pieces) - 1)),
                            reads=list(kkeys) + list(qkeys), writes=['st%d' % b])

                emit_scores(0)
                for idx, (kc0, nk, vblk, i0, deltas) in enumerate(kblocks):
                    if idx + 1 < nkb:
                        emit_scores(idx + 1)
                    b = bsel[idx]
                    o0 = qblocks[i0][0]
                    ncol = nq - o0
                    for j, dl in enumerate(deltas):
                        qo, qw = qblocks[i0 + j]
                        if dl > 0:
                            bias = biasTab_sb[:nk, hb, dl:dl + 1] if hb is not None else 0.0
                            S.op('act', lambda e, qo=qo, qw=qw, bias=bias: e.activation(
                                pt[:nk, b, qo:qo + qw], st_ps[:nk, b, qo:qo + qw], AF.Exp, bias=bias, scale=scale),
                                reads=['st%d' % b, 'btab'], writes=[('pt', b, j + i0)])
                        else:
                            S.op('act', lambda e, qo=qo, qw=qw: e.activation(
                                ptf[:nk, :qw], st_ps[:nk, b, qo:qo + qw], AF.Exp, scale=scale),
                                reads=['st%d' % b], writes=['ptf'])
                            S.op('dve', lambda e, qo=qo, qw=qw: e.tensor_tensor(
                                pt[:nk, b, qo:qo + qw], ptf[:nk, :qw], EB[:nk, :qw], op=ALU.mult),
                                reads=['ptf', 'EB'], writes=[('pt', b, j + i0)])
                    ptk = [('pt', b, j) for j in range(i0, len(qblocks))]
                    for vc in range(nvc):
                        S.op('pe', lambda e, vc=vc: e.matmul(
                            oacc[:, vc, o0:o0 + ncol], lhsT=V[:nk, vblk, vc * 128:(vc + 1) * 128], rhs=pt[:nk, b, o0:o0 + ncol],
                            start=(idx == 0), stop=(idx == nkb - 1), skip_group_check=True),
                            reads=ptk + ['V' + tag], writes=['oacc'])
                    S.op('pe', lambda e: e.matmul(
                        sacc[:, o0:o0 + ncol], lhsT=ones_b[:nk, :], rhs=pt[:nk, b, o0:o0 + ncol],
                        start=(idx == 0), stop=(idx == nkb - 1), skip_group_check=True),
                        reads=ptk + ['ones_b'], writes=['sacc'])
                S.op('dve', lambda e: e.reciprocal(rc[:, :nq], sacc[:, :nq]), reads=['sacc'], writes=['rc'])
                for vc in range(nvc):
                    S.op('dve', lambda e, vc=vc: e.tensor_tensor(On[:, vc, :nq], oacc[:, vc, :nq], rc[:, :nq], op=ALU.mult),
                         reads=['oacc', 'rc'], writes=[('On', vc)])
                out_cb(qc0, nq)

        def prompt_groups():
            gs = []
            for g in range(4):
                qblocks = [(i * 128, 128) for i in range(4)]
                kbl = []
                for kb in range(4 * g + 4):
                    i0 = max(0, kb - 4 * g)
                    kbl.append((kb * 128, 128, kb, i0, [4 * g + i - kb for i in range(i0, 4)]))
                gs.append((g * 512, qblocks, kbl))
            return gs

        def sample_groups():
            kbl = [(2080 + 128 * j, 128, 17 + j, 0, [8 - j]) for j in range(8)]
            kbl.append((2048, 32, 16, 0, [0]))
            return [(2048, [(0, 32)], kbl)]

        with ExitStack() as es:
            xblk = T(es, "xblkA", [128, 2, 2048], F32)
            tp = P(es, "tpA", [128, 16, 128])
            for (r0, n) in TB:
                b = S.rot('xblk', 2)
                S.dma('sp', xblk[:n, b, :], x_all[r0:r0 + n, :], writes=['xblk%d' % b])
                transpose_to(es, tp, xblk[:, b, :], n, 16, lambda a, c, r0=r0, n=n: AT[:, a:c, r0:r0 + n],
                             ['xblk%d' % b], ak(r0, n))
            S.barrier()
        chk('A')

        esM = es0.enter_context(ExitStack())
        cqnT = T(esM, "cqnT", [128, 4, NT], BF16)
        ckvnT = T(esM, "ckvnT", [128, 4, NKC], BF16)
        krT = T(esM, "krT", [64, NKC], BF16)
        with ExitStack() as es:
            Wd = T(es, "Wd", [128, 16, 1088], BF16)
            stg = T(es, "stgB", [128, 2, 1088], F32)
            gq = T(es, "gq", [128, 512], F32)
            gkv = T(es, "gkv", [128, 512], F32)
            cs_t = T(es, "cs_t", [128, 2, 64], F32)
            junk = T(es, "junk", [128, 512], F32)
            cqn = T(es, "cqn", [128, 512], F32)
            ckvn = T(es, "ckvn", [128, 2, 512], F32)
            krs = T(es, "krs", [128, 64], F32)
            kro = T(es, "kro", [128, 2, 64], F32)
            tmp4 = T(es, "tmp4", [128, 4, 32], F32)
            cblk = T(es, "cblk", [128, 2, 576], F32)
            lat = P(es, "lat", [128, 3, 512])
            tp = P(es, "tpB", [128, 16, 128])
            S.dma('sp', gq[:], gq_bc[:, :], writes=['gq'])
            S.dma('sp', gkv[:], gkv_bc[:, :], writes=['gkv'])
            for dc in range(16):
                b = S.rot('stgB', 2)
                S.dma('sp', stg[:, b, :], w_dqkv[dc * 128:(dc + 1) * 128, :], writes=['stgB%d' % b])
                S.op('pool', lambda e, dc=dc, b=b: e.tensor_copy(Wd[:, dc, :], stg[:, b, :]), reads=['stgB%d' % b], writes=['Wd'])
            for (r0, n) in TB:
                for j, (c0, w) in enumerate([(0, 512), (512, 512), (1024, 64)]):
                    for dc in range(16):
                        S.op('pe', lambda e, j=j, c0=c0, w=w, dc=dc: e.matmul(
                            lat[:n, j, :w], lhsT=AT[:, dc, r0:r0 + n], rhs=Wd[:, dc, c0:c0 + w], start=(dc == 0), stop=(dc == 15)),
                            reads=ak(r0, n) + ['Wd'], writes=['lat%d' % j])
                bb = S.rot('ckvn', 2)
                S.dma('sp', cs_t[:n, 0, 0:32], cosT[r0:r0 + n, :], writes=['cs_a'])
                S.dma('sp', cs_t[:n, 1, 0:32], sinT[r0:r0 + n, :], writes=['cs_b'])
                S.op('act', lambda e: e.activation(junk[:n, :], lat[:n, 0, :], AF.Square, accum_out=ss[:n, 0:1]), reads=['lat0'], writes=['junk', 'ss0'])
                S.op('act', lambda e: e.activation(junk[:n, :], lat[:n, 1, :], AF.Square, accum_out=ss[:n, 1:2]), reads=['lat1'], writes=['junk', 'ss1'])
                S.op('dve', lambda e: e.tensor_scalar(rs[:n, 0:2], ss[:n, 0:2], 1.0 / 512, RMS_EPS, op0=ALU.mult, op1=ALU.add), reads=['ss0', 'ss1'], writes=['rs'])
                S.op('act', lambda e: e.activation(rs[:n, 0:2], rs[:n, 0:2], AF.Sqrt), reads=['rs'], writes=['rs'])
                S.op('dve', lambda e: e.reciprocal(rs[:n, 0:2], rs[:n, 0:2]), reads=['rs'], writes=['rs'])
                S.op('dve', lambda e: e.scalar_tensor_tensor(cqn[:n, :], lat[:n, 0, :], rs[:n, 0:1], gq[:n, :], op0=ALU.mult, op1=ALU.mult),
                     reads=['lat0', 'rs', 'gq'], writes=['cqn'])
                S.op('dve', lambda e: e.scalar_tensor_tensor(ckvn[:n, bb, :], lat[:n, 1, :], rs[:n, 1:2], gkv[:n, :], op0=ALU.mult, op1=ALU.mult),
                     reads=['lat1', 'rs', 'gkv'], writes=['ckvn%d' % bb])
                S.dma('sp', o_ckv[r0:r0 + n, :], ckvn[:n, bb, :], reads=['ckvn%d' % bb], writes=['o_ckv'], sem='st_ckvn%d' % bb)
                S.op('act', lambda e: e.copy(krs[:n, :], lat[:n, 2, 0:64]), reads=['lat2'], writes=['krs'])
                S.op('dve', lambda e: e.tensor_tensor(tmp4[:n, 0, :], krs[:n, 0:32], cs_t[:n, 0, 0:32], op=ALU.mult), reads=['krs', 'cs_a'], writes=['t40'])
                S.op('dve', lambda e: e.tensor_tensor(tmp4[:n, 1, :], krs[:n, 32:64], cs_t[:n, 1, 0:32], op=ALU.mult), reads=['krs', 'cs_b'], writes=['t41'])
                S.op('dve', lambda e: e.tensor_tensor(tmp4[:n, 2, :], krs[:n, 32:64], cs_t[:n, 0, 0:32], op=ALU.mult), reads=['krs', 'cs_a'], writes=['t42'])
                S.op('dve', lambda e: e.tensor_tensor(tmp4[:n, 3, :], krs[:n, 0:32], cs_t[:n, 1, 0:32], op=ALU.mult), reads=['krs', 'cs_b'], writes=['t43'])
                S.op('dve', lambda e: e.tensor_tensor(kro[:n, bb, 0:32], tmp4[:n, 0, :], tmp4[:n, 1, :], op=ALU.subtract), reads=['t40', 't41'], writes=['kro%da' % bb])
                S.op('dve', lambda e: e.tensor_tensor(kro[:n, bb, 32:64], tmp4[:n, 2, :], tmp4[:n, 3, :], op=ALU.add), reads=['t42', 't43'], writes=['kro%db' % bb])
                S.dma('sp', o_kr[r0:r0 + n, :], kro[:n, bb, :], reads=['kro%da' % bb, 'kro%db' % bb], writes=['o_kr'], sem='st_kro%d' % bb)
                transpose_to(es, tp, cqn, n, 4, lambda a, c: cqnT[:, a:c, r0:r0 + n], ['cqn'], ['cqnT'])
                transpose_to(es, tp, ckvn[:, bb, :], n, 4, lambda a, c: ckvnT[:, a:c, r0:r0 + n], ['ckvn%d' % bb], ['ckvnT'])
                transpose_to(es, tp, kro[:, bb, :], n, 1, lambda a, c: krT[:64, r0:r0 + n].unsqueeze(1), ['kro%da' % bb, 'kro%db' % bb], ['krT'], np_=64)
            for j in range(8):
                b = S.rot('cblk', 2)
                S.dma('sp', cblk[:, b, 0:512], c_ckv[j * 128:(j + 1) * 128, :], writes=['cblk%d' % b])
                S.dma('sp', cblk[:, b, 512:576], c_kr[j * 128:(j + 1) * 128, :], writes=['cblk%d' % b])
                c0 = 2080 + j * 128
                transpose_to(es, tp, cblk[:, b, 0:512], 128, 4, lambda a, c: ckvnT[:, a:c, c0:c0 + 128], ['cblk%d' % b], ['ckvnT'])
                transpose_to(es, tp, cblk[:, b, 512:576], 128, 1, lambda a, c: krT[:64, c0:c0 + 128].unsqueeze(1), ['cblk%d' % b], ['krT'], np_=64)
            S.barrier()
        chk('B')

        with ExitStack() as es:
            stg = T(es, "stgC", [128, 4, 512], F32)
            wq_h = T(es, "wq_h", [128, 4, 192], BF16)
            wrs_h = T(es, "wrs_h", [128, 4, 64], BF16)
            wkv_h = T(es, "wkv_h", [128, 4, 256], BF16)
            cosF_sb = T(es, "cosF_sb", [64, NT], F32)
            sinF_sb = T(es, "sinF_sb", [64, NT], F32)
            EB_sb = T(es, "EBm_sb", [128, 128], F32)
            qnT = T(es, "qnT", [128, NT], BF16)
            qrT = T(es, "qrT", [64, NT], BF16)
            knT = T(es, "knT", [128, NKC], BF16)
            Vh = T(es, "Vh", [128, 25, 128], BF16)
            tm1 = T(es, "tm1", [64, 512], F32)
            tm2 = T(es, "tm2", [64, 512], F32)
            pt = T(es, "pt", [128, 2, 512], BF16)
            ptf = T(es, "ptf", [128, 128], F32)
            rc = T(es, "rc", [128, 512], F32)
            On = T(es, "On", [128, 2, 512], F32)
            st_ps = P(es, "st_ps", [128, 2, 512])
            oacc = P(es, "oacc", [128, 1, 512])
            sacc = P(es, "sacc", [128, 512])
            pq = P(es, "pq", [128, 2, 512])
            pr = P(es, "pr", [64, 512])
            prs = P(es, "prs", [64, 512])
            S.dma('sp', cosF_sb[:], cosF[:, :], writes=['cosF'])
            S.dma('sp', sinF_sb[:], sinF[:, :], writes=['sinF'])
            S.dma('sp', EB_sb[:], EBm[:, :], writes=['EB'])
            biasTab_sb = None
            KT = [(i * 512, 512) for i in range(6)] + [(3072, 32)]
            VB = [(i * 128, 128) for i in range(16)] + [(2048, 32)] + [(2080 + 128 * j, 128) for j in range(8)]
            for h in range(nheads0):
                S.dma('sp', stg[:, :, 0:192], w_uq.rearrange("(c p) n -> p c n", p=128)[:, :, h * 192:(h + 1) * 192], writes=['stgC0'])
                S.op('pool', lambda e: e.tensor_copy(wq_h[:], stg[:, :, 0:192]), reads=['stgC0'], writes=['wq_h'])
                S.dma('sp', stg[:, :, 192:256], w_uqs.rearrange("(c p) n -> p c n", p=128)[:, :, h * 64:(h + 1) * 64], writes=['stgC1'])
                S.op('pool', lambda e: e.tensor_copy(wrs_h[:], stg[:, :, 192:256]), reads=['stgC1'], writes=['wrs_h'])
                S.dma('sp', stg[:, :, 256:512], w_ukv.rearrange("(c p) n -> p c n", p=128)[:, :, h * 256:(h + 1) * 256], writes=['stgC2'])
                S.op('pool', lambda e: e.tensor_copy(wkv_h[:], stg[:, :, 256:512]), reads=['stgC2'], writes=['wkv_h'])
                for (t0, n) in TT:
                    pb_ = S.rot('pq', 2)
                    for ch in range(4):
                        S.op('pe', lambda e, ch=ch: e.matmul(pq[:, pb_, :n], lhsT=wq_h[:, ch, 0:128], rhs=cqnT[:, ch, t0:t0 + n], start=(ch == 0), stop=(ch == 3)),
                             reads=['wq_h', 'cqnT'], writes=['pq%d' % pb_])
                    S.op('act', lambda e: e.copy(qnT[:, t0:t0 + n], pq[:, pb_, :n]), reads=['pq%d' % pb_], writes=[('qnT', t0)])
                    for ch in range(4):
                        S.op('pe', lambda e, ch=ch: e.matmul(pr[:, :n], lhsT=wq_h[:, ch, 128:192], rhs=cqnT[:, ch, t0:t0 + n], start=(ch == 0), stop=(ch == 3)),
                             reads=['wq_h', 'cqnT'], writes=['pr'])
                    for ch in range(4):
                        S.op('pe', lambda e, ch=ch: e.matmul(prs[:, :n], lhsT=wrs_h[:, ch, :], rhs=cqnT[:, ch, t0:t0 + n], start=(ch == 0), stop=(ch == 3)),
                             reads=['wrs_h', 'cqnT'], writes=['prs'])
                    S.op('dve', lambda e: e.tensor_tensor(tm1[:, :n], pr[:, :n], cosF_sb[:, t0:t0 + n], op=ALU.mult), reads=['pr', 'cosF'], writes=['tm1'])
                    S.op('dve', lambda e: e.tensor_tensor(tm2[:, :n], prs[:, :n], sinF_sb[:, t0:t0 + n], op=ALU.mult), reads=['prs', 'sinF'], writes=['tm2'])
                    S.op('pool', lambda e: e.tensor_tensor(qrT[:, t0:t0 + n], tm1[:, :n], tm2[:, :n], op=ALU.add), reads=['tm1', 'tm2'], writes=['qrT'])
                for (t0, n) in KT:
                    pb_ = S.rot('pq', 2)
                    for ch in range(4):
                        S.op('pe', lambda e, ch=ch: e.matmul(pq[:, pb_, :n], lhsT=wkv_h[:, ch, 0:128], rhs=ckvnT[:, ch, t0:t0 + n], start=(ch == 0), stop=(ch == 3)),
                             reads=['wkv_h', 'ckvnT'], writes=['pq%d' % pb_])
                    S.op('act', lambda e: e.copy(knT[:, t0:t0 + n], pq[:, pb_, :n]), reads=['pq%d' % pb_], writes=[('knT', t0)])
                for bi, (k0, nk) in enumerate(VB):
                    pb_ = S.rot('pq', 2)
                    for ch in range(4):
                        S.op('pe', lambda e, ch=ch: e.matmul(pq[:nk, pb_, 0:128], lhsT=ckvnT[:, ch, k0:k0 + nk], rhs=wkv_h[:, ch, 128:256], start=(ch == 0), stop=(ch == 3)),
                             reads=['wkv_h', 'ckvnT'], writes=['pq%d' % pb_])
                    S.op('dve', lambda e: e.tensor_copy(Vh[:nk, bi, :], pq[:nk, pb_, 0:128]), reads=['pq%d' % pb_], writes=['Vm'])

                def cb(qc0, nq, h=h):
                    S.op('act', lambda e: e.copy(AT[:, h, qc0:qc0 + nq], On[:, 0, :nq]), reads=[('On', 0)], writes=ak(qc0, nq))
                pieces = [(knT, [('knT', t_) for t_, _ in KT], qnT, [('qnT', t_) for t_, _ in TT], 128), (krT, ['krT'], qrT, ['qrT'], 64)]
                attention(es, pieces, Vh, 1, prompt_groups() + sample_groups(), 192 ** -0.5, None, EB_sb,
                          (st_ps, oacc, sacc), (pt, ptf, rc, On), cb, 'm')
            S.barrier()
        esM.close()
        chk('CD')

        def wo_ln(w_o, res_src, lni, dst_d):
            with ExitStack() as es:
                Wo = T(es, "Wo", [128, 16, 2048], BF16)
                stg = T(es, "stgW", [128, 2, 2048], F32)
                gam = T(es, "gam", [128, 2048], F32)
                bet = T(es, "bet", [128, 2048], F32)
                xblk = T(es, "xblkW", [128, 2048], F32)
                y = T(es, "yW", [128, 2048], F32)
                xo = T(es, "xoW", [128, 2048], F32)
                mix = P(es, "mix", [128, 4, 512])
                tp = P(es, "tpW", [128, 16, 128])
                S.dma('sp', gam[:], ln_g[lni], writes=['lnp'])
                S.dma('sp', bet[:], ln_b[lni], writes=['lnp'])
                for ch in range(16):
                    b = S.rot('stgW', 2)
                    S.dma('sp', stg[:, b, :], w_o[ch * 128:(ch + 1) * 128, :], writes=['stgW%d' % b])
                    S.op('pool', lambda e, ch=ch, b=b: e.tensor_copy(Wo[:, ch, :], stg[:, b, :]), reads=['stgW%d' % b], writes=['Wo'])
                for (r0, n) in TB:
                    S.dma('sp', xblk[:n, :], res_src[r0:r0 + n, :], writes=['xblkW'])
                    for cs in range(4):
                        for ch in range(16):
                            S.op('pe', lambda e, cs=cs, ch=ch: e.matmul(mix[:n, cs, :], lhsT=AT[:, ch, r0:r0 + n], rhs=Wo[:, ch, cs * 512:(cs + 1) * 512],
                                                                      start=(ch == 0), stop=(ch == 15)),
                                 reads=ak(r0, n) + ['Wo'], writes=['mix%d' % cs])
                    S.op('dve', lambda e: e.scalar_tensor_tensor(y[:n, :], xblk[:n, :], ALPHA, mix[:n, :, :].rearrange("p a b -> p (a b)"),
                                                                 op0=ALU.mult, op1=ALU.add),
                         reads=['xblkW', 'mix0', 'mix1', 'mix2', 'mix3'], writes=['y'])
                    layer_norm(y, n, gam, bet, xo, 'y', 'xo')
                    S.dma('sp', dst_d[r0:r0 + n, :], xo[:n, :], reads=['xo'], writes=[('xa_d', r0)], sem='st_xoW')
                    transpose_to(es, tp, xo, n, 16, lambda a, c, r0=r0, n=n: AT[:, a:c, r0:r0 + n], ['xo'], ak(r0, n))
                S.barrier()

        def peer(l, lni, dst_d, final_out):
            with ExitStack() as es:
                stg = T(es, "stgQ", [128, 2, 16, 128], F32)
                wqc = T(es, "wqc", [128, 2, 16, 128], BF16)
                qT = T(es, "qTp", [128, 2, 16, 512], BF16)
                SKT = T(es, "SKT", [128, 16, 128], BF16)
                skf = stg[:, 0, :, :]
                Sst = T(es, "Sst", [128, 1, 128, 128], BF16)
                sv = T(es, "sv", [128, 16, 16], F32)
                si = T(es, "si", [128, 16, 16], U32)
                sif = T(es, "sif", [128, 16, 16], F32)
                scw = T(es, "scw", [128, 2, 128], F32)
                cand = T(es, "cand", [128, 8, 256], F32)
                cand2 = T(es, "cand2", [128, 8, 256], F32)
                fv = T(es, "fv", [128, 8, 16], F32)
                fi = T(es, "fi", [128, 8, 16], U32)
                fa = T(es, "fa", [128, 8, 16], U32)
                fb = T(es, "fb", [128, 8, 16], U32)
                faf = T(es, "faf", [128, 8, 16], F32)
                fbf = T(es, "fbf", [128, 8, 16], F32)
                I1 = T(es, "I1", [128, 128], F32)
                I2 = T(es, "I2", [128, 128], F32)
                gg = T(es, "gg", [128, 128], F32)
                ef = T(es, "ef", [128, 8, 16], F32)
                zz = T(es, "zz", [128, 8], F32)
                IT = T(es, "IT", [128, 3, 128], F32)
                An = T(es, "An", [128, 4, 128], BF16)
                Bn = T(es, "Bn", [128, 4, 128], BF16)
                pq = P(es, "pqR", [128, 512])
                sc = P(es, "sc", [128, 16, 128])
                tp = P(es, "tpR", [128, 3, 128])
                gp = P(es, "gp", [128, 2, 4, 128])
                S.dma('sp', skf, p_skT[l], writes=['stgQ0'])
                S.op('pool', lambda e: e.tensor_copy(SKT[:], skf), reads=['stgQ0'], writes=['SKT'])
                def qproj_cb(ti, cb):
                    (t0q, ntq) = TT[ti]
                    qb = ti % 2
                    b = S.rot('stgQ', 2)
                    S.dma('sp', stg[:, b, :, :], p_wq[l, cb], writes=['stgQ%d' % b])
                    S.op('act', lambda e: e.copy(wqc[:, b, :, :], stg[:, b, :, :]), reads=['stgQ%d' % b], writes=['wqc%d' % b])
                    for dc in range(16):
                        S.op('pe', lambda e, dc=dc: e.matmul(pq[:, :ntq], lhsT=wqc[:, b, dc, :], rhs=AT[:, dc, t0q:t0q + ntq], start=(dc == 0), stop=(dc == 15)),
                             reads=ak(t0q, ntq) + ['wqc%d' % b], writes=['pqR'])
                    S.op('act', lambda e: e.copy(qT[:, qb, cb, :ntq], pq[:, :ntq]), reads=['pqR'], writes=[('qTp', qb, cb)])

                for cb in range(16):
                    qproj_cb(0, cb)
                for ti, (t0, nt) in enumerate(TT):
                    qb = ti % 2
                    pending = list(range(16)) if ti + 1 < len(TT) else []
                    for bo in range(0, nt, 128):
                        n = min(128, nt - bo)
                        r0 = t0 + bo
                        sb_ = 0
                        for cb in range(16):
                            S.op('pe', lambda e, cb=cb: e.matmul(sc[:n, cb, :], lhsT=qT[:, qb, cb, bo:bo + n], rhs=SKT[:, cb, :], start=True, stop=True),
                                 reads=[('qTp', qb, cb), 'SKT'], writes=['sc'])
                        for cb0 in range(0, 16, 2):
                            for step in range(5):
                                for cb in (cb0, cb0 + 1):
                                    w_ = cb % 2
                                    if step == 0:
                                        S.op('dve', lambda e, cb=cb: e.max(sv[:n, cb, 0:8], sc[:n, cb, :]), reads=['sc'], writes=[('sva', cb)])
                                    elif step == 1:
                                        S.op('dve', lambda e, cb=cb: e.max_index(si[:n, cb, 0:8], sv[:n, cb, 0:8], sc[:n, cb, :]), reads=['sc', ('sva', cb)], writes=[('sia', cb)])
                                    elif step == 2:
                                        S.op('dve', lambda e, cb=cb: e.match_replace(scw[:n, w_, :], sv[:n, cb, 0:8], sc[:n, cb, :], -1e30), reads=['sc', ('sva', cb)], writes=[('scw', w_)])
                                    elif step == 3:
                                        S.op('dve', lambda e, cb=cb: e.max(sv[:n, cb, 8:16], scw[:n, w_, :]), reads=[('scw', w_)], writes=[('svb', cb)])
                                    else:
                                        S.op('dve', lambda e, cb=cb: e.max_index(si[:n, cb, 8:16], sv[:n, cb, 8:16], scw[:n, w_, :]), reads=[('scw', w_), ('svb', cb)], writes=[('sib', cb)])
                        svk = [('sva', c_) for c_ in range(16)] + [('svb', c_) for c_ in range(16)]
                        sik = [('sia', c_) for c_ in range(16)] + [('sib', c_) for c_ in range(16)]
                        S.op('dve', lambda e: e.tensor_copy(sif[:n], si[:n]), reads=sik, writes=['sif'])
                        sv4 = sv[:n].rearrange("p (h c) k -> p h c k", c=2)
                        sif4 = sif[:n].rearrange("p (h c) k -> p h c k", c=2)
                        c4 = cand[:n].rearrange("p h (a b) -> p h a b", b=16)
                        c24 = cand2[:n].rearrange("p h (a b) -> p h a b", b=16)
                        S.op('dve', lambda e: e.tensor_tensor(c4, sv4[:, :, 0, :].unsqueeze(3).to_broadcast([n, 8, 16, 16]),
                                                              sv4[:, :, 1, :].unsqueeze(2).to_broadcast([n, 8, 16, 16]), op=ALU.add),
                             reads=svk, writes=['cand'])
                        for h0 in range(0, 8, 2):
                            for step in range(5):
                                for h in (h0, h0 + 1):
                                    if step == 0:
                                        S.op('dve', lambda e, h=h: e.max(fv[:n, h, 0:8], cand[:n, h, :]), reads=['cand'], writes=[('fva', h)])
                                    elif step == 1:
                                        S.op('dve', lambda e, h=h: e.max_index(fi[:n, h, 0:8], fv[:n, h, 0:8], cand[:n, h, :]), reads=['cand', ('fva', h)], writes=[('fia', h)])
                                    elif step == 2:
                                        S.op('dve', lambda e, h=h: e.match_replace(cand2[:n, h, :], fv[:n, h, 0:8], cand[:n, h, :], -1e30), reads=['cand', ('fva', h)], writes=[('cand2', h)])
                                    elif step == 3:
                                        S.op('dve', lambda e, h=h: e.max(fv[:n, h, 8:16], cand2[:n, h, :]), reads=[('cand2', h)], writes=[('fvb', h)])
                                    else:
                                        S.op('dve', lambda e, h=h: e.max_index(fi[:n, h, 8:16], fv[:n, h, 8:16], cand2[:n, h, :]), reads=[('cand2', h), ('fvb', h)], writes=[('fib', h)])
                        fvk = [('fva', h_) for h_ in range(8)] + [('fvb', h_) for h_ in range(8)]
                        fik = [('fia', h_) for h_ in range(8)] + [('fib', h_) for h_ in range(8)]
                        c2k = [('cand2', h_) for h_ in range(8)]
                        S.op('dve', lambda e: e.tensor_single_scalar(fa[:n], fi[:n], 4, op=ALU.logical_shift_right), reads=fik, writes=['fa'])
                        S.op('dve', lambda e: e.tensor_single_scalar(fb[:n], fi[:n], 15, op=ALU.bitwise_and), reads=fik, writes=['fb'])
                        S.op('dve', lambda e: e.tensor_copy(faf[:n], fa[:n]), reads=['fa'], writes=['faf'])
                        S.op('dve', lambda e: e.tensor_copy(fbf[:n], fb[:n]), reads=['fb'], writes=['fbf'])
                        io4 = iota16[:n, :].unsqueeze(1).unsqueeze(1).to_broadcast([n, 8, 16, 16])
                        for (idxf, cc, dst, kk) in ((faf, 0, I1, 'I1'), (fbf, 1, I2, 'I2')):
                            S.op('dve', lambda e, idxf=idxf: e.tensor_tensor(c4, io4, idxf[:n].unsqueeze(3).to_broadcast([n, 8, 16, 16]), op=ALU.is_equal),
                                 reads=['iota16', 'faf', 'fbf'], writes=['cand'])
                            S.op('dve', lambda e, cc=cc: e.tensor_tensor(c24, c4, sif4[:, :, cc, :].unsqueeze(2).to_broadcast([n, 8, 16, 16]), op=ALU.mult),
                                 reads=['cand', 'sif'], writes=c2k + ['cand2'])
                            S.op('dve', lambda e, dst=dst: e.tensor_reduce(dst[:n, :].rearrange("p (h k) -> p h k", k=16), c24, axis=AX.X, op=ALU.add),
                                 reads=c2k + ['cand2'], writes=[kk])
                        S.op('dve', lambda e: e.tensor_tensor(ef[:n], fv[:n], fv[:n, :, 0:1].to_broadcast([n, 8, 16]), op=ALU.subtract), reads=fvk, writes=['ef'])
                        S.op('act', lambda e: e.activation(ef[:n], ef[:n], AF.Exp), reads=['ef'], writes=['ef'])
                        S.op('dve', lambda e: e.tensor_reduce(zz[:n, :], ef[:n], axis=AX.X, op=ALU.add), reads=['ef'], writes=['zz'])
                        S.op('dve', lambda e: e.reciprocal(zz[:n, :], zz[:n, :]), reads=['zz'], writes=['zz'])
                        S.op('dve', lambda e: e.tensor_tensor(gg[:n, :].rearrange("p (h k) -> p h k", k=16), ef[:n], zz[:n, :].unsqueeze(2).to_broadcast([n, 8, 16]), op=ALU.mult),
                             reads=['ef', 'zz'], writes=['gg'])
                        for j, (src, kk) in enumerate(((I1, 'I1'), (I2, 'I2'), (gg, 'gg'))):
                            S.op('pe', lambda e, j=j, src=src: e.transpose(tp[:, j, :n], src[:n, :], ident[:n, :n]), reads=[kk, 'ident'], writes=['tpR'])
                        S.op('act', lambda e: e.copy(IT[:, :, :n], tp[:, :, :n]), reads=['tpR'], writes=['IT'])
                        for tk in range(n):
                            q = S.rot('AB', 4)
                            S.op('dve', lambda e, q=q, tk=tk: e.tensor_scalar(An[:, q, :], iotaF[:, :], IT[:, 0, tk:tk + 1], IT[:, 2, tk:tk + 1],
                                                                             op0=ALU.is_equal, op1=ALU.mult),
                                 reads=['IT', 'iotaF'], writes=['An%d' % q])
                            S.op('dve', lambda e, q=q, tk=tk: e.tensor_scalar(Bn[:, q, :], iotaF[:, :], IT[:, 1, tk:tk + 1], None, op0=ALU.is_equal),
                                 reads=['IT', 'iotaF'], writes=['Bn%d' % q])
                            gb = (tk // 4) % 2
                            S.op('pe', lambda e, q=q, tk=tk, gb=gb: e.matmul(gp[:, gb, tk % 4, :], lhsT=Bn[:, q, :], rhs=An[:, q, :], start=True, stop=True),
                                 reads=['An%d' % q, 'Bn%d' % q], writes=['gp%d' % gb])
                            if tk % 4 == 3 or tk == n - 1:
                                m = tk % 4 + 1
                                tk0 = tk - (m - 1)
                                S.op('act', lambda e, gb=gb, m=m, tk0=tk0: e.copy(Sst[:, sb_, :, tk0:tk0 + m], gp[:, gb, 0:m, :].rearrange("p t i -> p i t")),
                                     reads=['gp%d' % gb], writes=['Sst%d' % sb_])
                            if tk % 32 == 31 and pending:
                                qproj_cb(ti + 1, pending.pop(0))
                        segs = []
                        a0 = r0
                        while a0 < r0 + n:
                            tl = a0 // TILE
                            a1 = min(r0 + n, (tl + 1) * TILE)
                            segs.append((tl, a0 - tl * TILE, a0 - r0, a1 - a0))
                            a0 = a1
                        for (tl, off, s0, ln) in segs:
                            for i1a in range(0, 128, 16):
                                S.dma('sp', GT[tl].rearrange("a b n -> b a n")[:, i1a:i1a + 16, off:off + ln], Sst[:, sb_, i1a:i1a + 16, s0:s0 + ln],
                                      reads=['Sst%d' % sb_], writes=[('GT', tl)], sem='st_Sst%d' % sb_)
                    while pending:
                        qproj_cb(ti + 1, pending.pop(0))
                S.barrier()
            chk('R%d' % l)
            with ExitStack() as es:
                acc = T(es, "acc", [128, 9, 2048], F32)
                hp = P(es, "hp", [128, 3, 512])
                op_ = P(es, "op", [128, 2, 1024])
                xa_src = xa_d
                for tl in range(2):
                    esd = es.enter_context(ExitStack())
                    stg = T(esd, "stgD", [128, 2, 2048], F32)
                    uTb = T(esd, "uTb", [128, 2, 16, 128], BF16)
                    vb = T(esd, "vb", [128, 2, 2, 2048], BF16)
                    gt = T(esd, "gt", [128, 2, TILE], BF16)
                    hg = T(esd, "hg", [128, 2, TILE], BF16)
                    wT = T(esd, "wT", [128, 2, 2, TILE], BF16)
                    t0 = tl * TILE
                    LB = [(i * 128, 128) for i in range(8)] + [(1024, 16)]
                    NTL = [(0, 512), (512, 512), (1024, 16)]
                    for bi, (o, n) in enumerate(LB):
                        S.dma('sp', acc[:n, bi, :], xa_src[t0 + o:t0 + o + n, :], reads=[('xa_d', 0)], writes=[('acc', bi)])
                        S.op('pool', lambda e, bi=bi, n=n: e.tensor_scalar(acc[:n, bi, :], acc[:n, bi, :], ALPHA, 0.0, op0=ALU.mult, op1=ALU.add),
                             reads=[('acc', bi)], writes=[('acc', bi)])
                    info = {}
                    pre_U = set()

                    def loads(i1):
                        sU = S.rot("stgD", 2)
                        ub = S.rot('uTb', 2)
                        S.dma('sp', stg[:, sU, :], p_uT[l, i1].rearrange("p c e -> p (c e)"), writes=['stgD%d' % sU])
                        S.op('pool', lambda e: e.tensor_copy(uTb[:, ub, :, :].rearrange("p c e -> p (c e)"), stg[:, sU, :]),
                             reads=['stgD%d' % sU], writes=['uTb%d' % ub])
                        sV = S.rot("stgD", 2)
                        grp = (i1 // 2) % 2
                        slot = i1 % 2
                        S.dma('sp', stg[:, sV, :], p_v[l, i1 * 128:(i1 + 1) * 128, :], writes=['stgD%d' % sV])
                        S.op('act', lambda e: e.copy(vb[:, grp, slot, :], stg[:, sV, :]),
                             reads=['stgD%d' % sV], writes=[('vb', grp, slot)])
                        gb = S.rot('gt', 2)
                        S.dma('sp', gt[:, gb, :], GT[tl, i1], reads=[('GT', tl)], writes=['gt%d' % gb])
                        info[i1] = (ub, grp, slot, gb)

                    def u_closures(i1):
                        ub = info[i1][0]
                        cl = []
                        for ni, (c0, w) in enumerate(NTL):
                            for dc in range(16):
                                cl.append(lambda ni=ni, c0=c0, w=w, dc=dc: S.op('pe', lambda e: e.matmul(
                                    hp[:, ni, :w], lhsT=uTb[:, ub, dc, :], rhs=AT[:, dc, t0 + c0:t0 + c0 + w], start=(dc == 0), stop=(dc == 15)),
                                    reads=['uTb%d' % ub] + ak(t0 + c0, w), writes=['hp%d' % ni]))
                        return cl

                    def post(i1):
                        (ub, grp, slot, gb) = info[i1]
                        hb_ = S.rot('hg', 2)
                        for ni, (c0, w) in enumerate(NTL):
                            S.op('act', lambda e, ni=ni, c0=c0, w=w: e.activation(hg[:, hb_, c0:c0 + w], hp[:, ni, :w], AF.Gelu_apprx_tanh),
                                 reads=['hp%d' % ni], writes=[('hg', hb_, ni)])
                        S.op('dve', lambda e: e.tensor_tensor(wT[:, grp, slot, :], hg[:, hb_, :], gt[:, gb, :], op=ALU.mult),
                             reads=[('hg', hb_, 0), ('hg', hb_, 1), ('hg', hb_, 2), 'gt%d' % gb], writes=[('wT', grp, slot)])

                    for i1 in range(ni1):
                        if i1 not in info:
                            loads(i1)
                        if i1 not in pre_U:
                            for c in u_closures(i1):
                                c()
                        post(i1)
                        (ub, grp, slot, gb) = info[i1]
                        if slot == 1:
                            nxt = []
                            if i1 + 1 < ni1:
                                loads(i1 + 1)
                                nxt = u_closures(i1 + 1)
                                pre_U.add(i1 + 1)
                            for bi, (o, n) in enumerate(LB):
                                for half in range(2):
                                    pb = S.rot('op', 2)
                                    for cs in range(2):
                                        for s_ in range(2):
                                            cc = (half * 2 + cs) * 512
                                            S.op('pe', lambda e, cs=cs, s_=s_, cc=cc: e.matmul(
                                                op_[:n, pb, cs * 512:(cs + 1) * 512], lhsT=wT[:, grp, s_, o:o + n], rhs=vb[:, grp, s_, cc:cc + 512],
                                                start=(s_ == 0), stop=(s_ == 1)),
                                                reads=[('wT', grp, s_), ('vb', grp, s_)], writes=['op%d_%d' % (pb, cs)])
                                    S.op('dve', lambda e, half=half: e.tensor_tensor(
                                        acc[:n, bi, half * 1024:(half + 1) * 1024], acc[:n, bi, half * 1024:(half + 1) * 1024], op_[:n, pb, :], op=ALU.add),
                                        reads=['op%d_0' % pb, 'op%d_1' % pb, ('acc', bi)], writes=[('acc', bi)])
                                    for _ in range(3):
                                        if nxt:
                                            nxt.pop(0)()
                            while nxt:
                                nxt.pop(0)()
                    S.barrier()
                    esd.close()
                    with ExitStack() as es2:
                        xo = T(es2, "xoD", [128, 2048], F32)
                        gam = T(es2, "gamD", [128, 2048], F32)
                        bet = T(es2, "betD", [128, 2048], F32)
                        S.dma('sp', gam[:], ln_g[lni], writes=['lnp'])
                        S.dma('sp', bet[:], ln_b[lni], writes=['lnp'])
                        tpv = op_[:, :, :].rearrange("p a (c t) -> p (a c) t", t=128)
                        opk = ['op0_0', 'op0_1', 'op1_0', 'op1_1']
                        for bi, (o, n) in enumerate(LB):
                            r0 = t0 + o
                            layer_norm(acc[:, bi, :], n, gam, bet, xo, ('acc', bi), 'xo')
                            if final_out:
                                S.dma('sp', o_y[r0:r0 + n, :], xo[:n, :], reads=['xo'], writes=['o_y'], sem='st_xoD')
                            else:
                                S.dma('sp', dst_d[r0:r0 + n, :], xo[:n, :], reads=['xo'], writes=[('x1_d', 0)], sem='st_xoD')
                                for ch in range(16):
                                    S.op('pe', lambda e, ch=ch: e.transpose(tpv[:, ch, :n], xo[:n, ch * 128:(ch + 1) * 128], ident[:n, :n]),
                                         reads=['xo', 'ident'], writes=opk)
                                S.op('act', lambda e: e.copy(AT[:, 0:8, r0:r0 + n], tpv[:, 0:8, :n]), reads=opk, writes=ak(r0, n) + ['_x'])
                                S.op('dve', lambda e: e.tensor_copy(AT[:, 8:16, r0:r0 + n], tpv[:, 8:16, :n]), reads=opk, writes=ak(r0, n) + ['_y'])
                    S.barrier()

        wo_ln(w_o0, x_all, 0, xa_d)
        chk('E0')
        peer(0, 1, x1_d, False)
        chk('P0')

        with ExitStack() as es:
            oth = T(es, "oth", [128, 2, NT], BF16)
            stg = T(es, "stgE", [128, 16, 256], F32)
            wh = T(es, "wh", [128, 3, 16, 256], BF16)
            qkT = T(es, "qkT", [128, 2, NT], BF16)
            kkT = T(es, "kkT", [128, 2, NKC], BF16)
            Vd = T(es, "Vd", [128, 25, 256], BF16)
            kvo = T(es, "kvo", [128, 2, 256], F32)
            cb_t = T(es, "cb_t", [128, 2, 256], F32)
            EB_sb = T(es, "EBd_sb", [128, 8, 128], F32)
            btab = T(es, "btab", [128, 8, 17], F32)
            lam_t = T(es, "lam_t", [128, 4, 128], F32)
            lam_s = T(es, "lam_s", [128, 4], F32)
            gs_t = T(es, "gs_t", [128, 2], F32)
            pt = T(es, "ptd", [128, 2, 512], BF16)
            ptf = T(es, "ptfd", [128, 128], F32)
            rc = T(es, "rcd", [128, 512], F32)
            On = T(es, "Ond", [128, 2, 512], F32)
            O1 = T(es, "O1", [128, 2, NT], F32)
            sq = T(es, "sq", [128, 2, 512], BF16)
            st_ps = P(es, "st_psd", [128, 2, 512])
            oacc = P(es, "oaccd", [128, 2, 512])
            sacc = P(es, "saccd", [128, 512])
            pq = P(es, "pqd", [128, 512])
            tp = P(es, "tpd", [128, 2, 128])
            biasTab_sb = btab
            S.dma('sp', EB_sb[:], EBd[:, :, :], writes=['EB'])
            S.dma('sp', btab[:], biasTab[:, :, :], writes=['btab'])
            S.dma('sp', lam_t[:], lamv[:, :, :], writes=['lam_t'])
            S.dma('sp', gs_t[:], gsub[:, :], writes=['gs_t'])
            S.op('dve', lambda e: e.tensor_tensor(lam_t[:, 0, :], lam_t[:, 0, :], lam_t[:, 1, :], op=ALU.mult), reads=['lam_t'], writes=['lam_t'])
            S.op('dve', lambda e: e.tensor_tensor(lam_t[:, 2, :], lam_t[:, 2, :], lam_t[:, 3, :], op=ALU.mult), reads=['lam_t'], writes=['lam_t'])
            S.op('dve', lambda e: e.tensor_reduce(lam_s[:, 0:1], lam_t[:, 0, :], axis=AX.X, op=ALU.add), reads=['lam_t'], writes=['lam_s'])
            S.op('dve', lambda e: e.tensor_reduce(lam_s[:, 1:2], lam_t[:, 2, :], axis=AX.X, op=ALU.add), reads=['lam_t'], writes=['lam_s'])
            S.op('act', lambda e: e.activation(lam_s[:, 0:2], lam_s[:, 0:2], AF.Exp), reads=['lam_s'], writes=['lam_s'])
            S.op('dve', lambda e: e.tensor_tensor(lam_s[:, 2:3], lam_s[:, 0:1], lam_s[:, 1:2], op=ALU.subtract), reads=['lam_s'], writes=['lam_s'])
            S.op('dve', lambda e: e.tensor_scalar(lam_s[:, 3:4], lam_s[:, 2:3], LAM_INIT, -1.0, op0=ALU.add, op1=ALU.mult), reads=['lam_s'], writes=['lam_s'])
            S.op('dve', lambda e: e.tensor_scalar(gs_t[:, :], gs_t[:, :], 1.0 - LAM_INIT, None, op0=ALU.mult), reads=['gs_t'], writes=['gs_t'])
            wv = w_qkv.rearrange("(c p) n -> p c n", p=128)
            KT = [(i * 512, 512) for i in range(4)] + [(2048, 32)]
            VB = [(i * 128, 128) for i in range(16)] + [(2048, 32)] + [(2080 + 128 * j, 128) for j in range(8)]
            for h in range(nheads1):
                for j, c0 in enumerate((h * 256, 2048 + h * 256, 4096 + h * 256)):
                    S.dma('sp', stg[:, :, :], wv[:, :, c0:c0 + 256], writes=['stgE'])
                    S.op(('act', 'dve', 'pool')[j], (lambda e, j=j: e.copy(wh[:, j, :, :], stg[:, :, :])) if j == 0 else (lambda e, j=j: e.tensor_copy(wh[:, j, :, :], stg[:, :, :])), reads=['stgE'], writes=['wh%d' % j])
                chk('D1')
                import os as _os
                _dj = _os.environ.get('DBG_J')
                for j, dstT in ((0, qkT), (1, kkT)):
                    if _dj is not None and str(j) not in _dj:
                        continue
                    for s2 in range(2):
                        for (t0, n) in KT:
                            for dc in range(16):
                                S.op('pe', lambda e, j=j, s2=s2, dc=dc: e.matmul(pq[:, :n], lhsT=wh[:, j, dc, s2 * 128:(s2 + 1) * 128], rhs=AT[:, dc, t0:t0 + n],
                                                                                 start=(dc == 0), stop=(dc == 15)),
                                     reads=['wh%d' % j] + ak(t0, n), writes=['pqd'])
                            S.op('act', lambda e, dstT=dstT, s2=s2: e.copy(dstT[:, s2, t0:t0 + n], pq[:, :n]), reads=['pqd'], writes=['qkT' if j == 0 else 'kkT'])
                chk('D2')
                for bi, (r0, n) in enumerate(TB):
                    for j, (od, kk) in ((1, (o_dk, 0)), (2, (o_dv, 1))):
                        for dc in range(16):
                            S.op('pe', lambda e, j=j, dc=dc: e.matmul(pq[:n, 0:256], lhsT=AT[:, dc, r0:r0 + n], rhs=wh[:, j, dc, :], start=(dc == 0), stop=(dc == 15)),
                                 reads=['wh%d' % j] + ak(r0, n), writes=['pqd'])
                        S.op('act', lambda e, kk=kk: e.copy(kvo[:n, kk, :], pq[:n, 0:256]), reads=['pqd'], writes=['kvo%d' % kk])
                        if kk == 1:
                            S.op('dve', lambda e: e.tensor_copy(Vd[:n, bi, :], kvo[:n, 1, :]), reads=['kvo1'], writes=['Vd'])
                        if _os.environ.get('DBG_NOKV') is None:
                            S.dma('sp', od[r0:r0 + n, h * 256:(h + 1) * 256], kvo[:n, kk, :], reads=['kvo%d' % kk], writes=['o_dkv%d' % kk], sem='st_kvo%d' % kk)
                chk('D3')
                for jb in range(8):
                    S.dma('sp', cb_t[:, 0, :], c_dk[jb * 128:(jb + 1) * 128, h * 256:(h + 1) * 256], writes=['cbt0'])
                    S.dma('sp', cb_t[:, 1, :], c_dv[jb * 128:(jb + 1) * 128, h * 256:(h + 1) * 256], writes=['cbt1'])
                    for s2 in range(2):
                        S.op('pe', lambda e, s2=s2: e.transpose(tp[:, s2, :], cb_t[:, 0, s2 * 128:(s2 + 1) * 128], ident[:, :]), reads=['cbt0', 'ident'], writes=['tpd'])
                    S.op('act', lambda e: e.copy(kkT[:, :, 2080 + jb * 128:2080 + (jb + 1) * 128], tp[:, :, :]), reads=['tpd'], writes=['kkT'])
                    S.op('dve', lambda e: e.tensor_copy(Vd[:, 17 + jb, :], cb_t[:, 1, :]), reads=['cbt1'], writes=['Vd'])
                chk('D4')
                for s2 in range(2):
                    def cb(qc0, nq, s2=s2, h=h):
                        if s2 == 0:
                            S.op('pool', lambda e: e.tensor_copy(O1[:, :, qc0:qc0 + nq], On[:, :, :nq]), reads=[('On', 0), ('On', 1)], writes=['O1'])
                        else:
                            for vc in range(2):
                                S.op('dve', lambda e, vc=vc: e.scalar_tensor_tensor(On[:, vc, :nq], On[:, vc, :nq], lam_s[:, 3:4], O1[:, vc, qc0:qc0 + nq], op0=ALU.mult, op1=ALU.add),
                                     reads=[('On', vc), 'O1', 'lam_s'], writes=[('On', vc)])
                                S.op('act', lambda e, vc=vc: e.activation(sq[:, vc, :nq], On[:, vc, :nq], AF.Square), reads=[('On', vc)], writes=[('sq', vc)])
                            for vc in range(2):
                                S.op('pe', lambda e, vc=vc: e.matmul(sacc[:, :nq], lhsT=ones_b[:, :], rhs=sq[:, vc, :nq], start=(vc == 0), stop=(vc == 1)),
                                     reads=[('sq', vc), 'ones_b'], writes=['sacc'])
                            S.op('dve', lambda e: e.tensor_scalar(rc[:, :nq], sacc[:, :nq], 1.0 / 256, RMS_EPS, op0=ALU.mult, op1=ALU.add), reads=['sacc'], writes=['rc'])
                            S.op('act', lambda e: e.activation(rc[:, :nq], rc[:, :nq], AF.Sqrt), reads=['rc'], writes=['rc'])
                            S.op('dve', lambda e: e.reciprocal(rc[:, :nq], rc[:, :nq]), reads=['rc'], writes=['rc'])
                            for vc in range(2):
                                S.op('dve', lambda e, vc=vc: e.scalar_tensor_tensor(oth[:, vc, qc0:qc0 + nq], On[:, vc, :nq], gs_t[:, vc:vc + 1], rc[:, :nq],
                                                                                    op0=ALU.mult, op1=ALU.mult),
                                     reads=[('On', vc), 'rc', 'gs_t'], writes=['oth'])
                    pieces = [(kkT[:, s2, :], ['kkT'], qkT[:, s2, :], ['qkT'], 128)]
                    attention(es, pieces, Vd, 2, prompt_groups() + sample_groups(), 128 ** -0.5, h, EB_sb[:, h, :],
                              (st_ps, oacc, sacc), (pt, ptf, rc, On), cb, 'd')
                chk('D5')
                for vc in range(2):
                    S.dma('sp', ot_d[2 * h + vc], oth[:, vc, :], reads=['oth'], writes=['ot_d'], sem='st_oth')
            S.barrier()
            for c in range(16):
                S.dma('sp', AT[:, c, :], ot_d[c], reads=['ot_d'], writes=[('ATall', c)], sem='ds_ATall')
            S.barrier()

        chk('DF')
        wo_ln(w_o1, x1_d, 2, xa_d)
        chk('E1')
        peer(1, 3, None, True)
        S.barrier()
    except _Stop:
        pass
    return nc


_NC = None


def _host_consts():
    c = {}
    c["c_ident"] = np.eye(128, dtype=np.float32)
    c["c_iota"] = np.tile(np.arange(128, dtype=np.float32)[None, :], (128, 1))
    c["c_iota16"] = np.tile(np.arange(16, dtype=np.float32)[None, :], (128, 1))
    pos = np.concatenate([np.arange(2048), 1024 + np.arange(32)]).astype(np.float32)
    half = 32
    inv = (10000.0 ** (-np.arange(half, dtype=np.float32) / half)).astype(np.float32)
    ang = (pos[:, None] * inv[None, :]).astype(np.float32)
    cos = np.cos(ang).astype(np.float32)
    sin = np.sin(ang).astype(np.float32)
    c["cosT"] = cos
    c["sinT"] = sin
    c["cosF"] = np.ascontiguousarray(np.concatenate([cos, cos], 1).T)
    c["sinF"] = np.ascontiguousarray(np.concatenate([-sin, sin], 1).T)
    ki = np.arange(128)[:, None]
    qi = np.arange(128)[None, :]
    vis = ((ki // 64) <= (qi // 64)).astype(np.float64)
    c["EBm"] = vis.astype(np.float32)
    slopes = 2.0 ** (-8.0 * np.arange(1, 9) / 8)
    EBd = np.zeros((128, 8, 128), np.float64)
    for h in range(8):
        EBd[:, h, :] = np.exp(-slopes[h] * np.abs(qi - ki) + slopes[h] * qi) * vis
    c["EBd"] = EBd.astype(np.float32)
    bt = np.zeros((128, 8, 17), np.float64)
    for h in range(8):
        for d in range(17):
            bt[:, h, d] = -slopes[h] * 128.0 * d + slopes[h] * np.arange(128)
    c["biasTab"] = bt.astype(np.float32)
    return c


def kernel(x_prompt, x_sample, cache_mla_ckv, cache_mla_krope, cache_diff_k, cache_diff_v,
           mla_w_dqkv, mla_g_q, mla_w_uq, mla_g_kv, mla_w_ukv, mla_w_o,
           diff_w_qkv, diff_lam_q1, diff_lam_k1, diff_lam_q2, diff_lam_k2, diff_g_sub, diff_w_o,
           peer_w_query, peer_sub_keys, peer_u, peer_v,
           ln_mix_g, ln_mix_b, ln_ffn_g, ln_ffn_b):
    global _NC
    f = lambda a: np.ascontiguousarray(np.asarray(a, dtype=np.float32))
    x_prompt, x_sample = f(x_prompt), f(x_sample)
    if _NC is None:
        _NC = build_nc()
    nc = _NC
    consts = _host_consts()
    w_uq = f(mla_w_uq)[0]
    wr = w_uq.reshape(512, 16, 192)[:, :, 128:192]
    w_uqs = np.ascontiguousarray(np.concatenate([wr[:, :, 32:64], wr[:, :, 0:32]], -1).reshape(512, 1024))
    p_skT = np.ascontiguousarray(np.transpose(f(peer_sub_keys).reshape(2, 16, 128, 128), (0, 3, 1, 2)))
    pu = f(peer_u).reshape(2, 128, 128, 16, 128)
    p_uT = np.ascontiguousarray(np.transpose(pu, (0, 1, 4, 3, 2)))
    bc = lambda v, n: np.ascontiguousarray(np.broadcast_to(f(v).reshape(1, n), (128, n)))
    lng = np.stack([bc(ln_mix_g[0], 2048), bc(ln_ffn_g[0], 2048), bc(ln_mix_g[1], 2048), bc(ln_ffn_g[1], 2048)])
    lnb = np.stack([bc(ln_mix_b[0], 2048), bc(ln_ffn_b[0], 2048), bc(ln_mix_b[1], 2048), bc(ln_ffn_b[1], 2048)])
    gsub = np.ascontiguousarray(f(diff_g_sub)[0].reshape(2, 128).T)
    lamv = np.ascontiguousarray(np.stack([bc(diff_lam_q1[0], 128), bc(diff_lam_k1[0], 128), bc(diff_lam_q2[0], 128), bc(diff_lam_k2[0], 128)], 1))
    shared = dict(
        w_dqkv=f(mla_w_dqkv)[0], w_uq=w_uq, w_uqs=w_uqs, w_ukv=f(mla_w_ukv)[0], w_o0=f(mla_w_o)[0],
        w_qkv=f(diff_w_qkv)[0], w_o1=f(diff_w_o)[0], p_wq=np.ascontiguousarray(np.transpose(f(peer_w_query).reshape(2, 16, 128, 16, 128), (0, 3, 2, 1, 4))), p_skT=p_skT, p_uT=p_uT, p_v=f(peer_v),
        gq_bc=bc(mla_g_q[0], 512), gkv_bc=bc(mla_g_kv[0], 512), ln_g=lng, ln_b=lnb, gsub=gsub, lamv=lamv, **consts)
    in_maps = []
    for c in range(8):
        m = dict(shared)
        m["x_all"] = np.ascontiguousarray(np.concatenate([x_prompt[c // 2], x_sample[c]], 0))
        m["c_ckv"] = f(cache_mla_ckv)[0, c]
        m["c_kr"] = f(cache_mla_krope)[0, c]
        m["c_dk"] = np.ascontiguousarray(f(cache_diff_k)[0, c].reshape(1024, 2048))
        m["c_dv"] = np.ascontiguousarray(f(cache_diff_v)[0, c].reshape(1024, 2048))
        in_maps.append(m)
    if _NI1[0] < 128:
        shared['p_uT'] = np.ascontiguousarray(shared['p_uT'][:, :max(_NI1[0], 1)])
        shared['p_v'] = np.ascontiguousarray(shared['p_v'][:, :max(_NI1[0], 1) * 128])
        for m in in_maps:
            m['p_uT'] = shared['p_uT']
            m['p_v'] = shared['p_v']
    in_maps = [{k: v for k, v in m.items() if k in _DECL} for m in in_maps]
    res = run_bass_kernel_spmd(nc, in_maps, core_ids=list(range(8)))
    R = res.results
    pr = lambda k, w: np.stack([R[2 * b][k][:2048] for b in range(4)])
    sm = lambda k: np.stack([R[c][k][2048:2080] for c in range(8)])
    y_p = pr("o_y", 2048)
    y_s = sm("o_y")
    return (y_p, y_s,
            pr("o_ckv", 512)[None], pr("o_kr", 64)[None],
            pr("o_dk", 2048).reshape(1, 4, 2048, 8, 256), pr("o_dv", 2048).reshape(1, 4, 2048, 8, 256),
            sm("o_ckv")[None], sm("o_kr")[None],
            sm("o_dk").reshape(1, 8, 32, 8, 256), sm("o_dv").reshape(1, 8, 32, 8, 256))
```
